# Optimizing a Trainium2 kernel written in Bass

```python
import jax, jax.numpy as jnp
from jax import lax
import numpy as np


D_MODEL = 4096
BATCH = 4
SEQ = 2048
DEPTH = 1

GM_WIDTH = 2048
GM_GROUPS = 8
GM_GROUP_W = GM_WIDTH // GM_GROUPS
CHUNK = 128
MLA_HEADS = 32
QK_NOPE = 128
QK_ROPE = 64
QK_HEAD = QK_NOPE + QK_ROPE
V_HEAD = 128
Q_LORA = 1024
KV_LORA = 512
ROPE_THETA = 10000.0
Q_BLOCK = 128
D_FF = 4 * D_MODEL
N_BRANCH = 2
N_MOD = 6
EPS = 1e-6
OFF_Q = 2 * GM_WIDTH
OFF_KV = OFF_Q + Q_LORA
OFF_KPE = OFF_KV + KV_LORA
OFF_GATE = OFF_KPE + QK_ROPE
IN_COLS = OFF_GATE + N_BRANCH * D_MODEL

kernel_name = 'hybrid_gmlp_mla_block'


def rms_norm(x, g):
    xf = x.astype(jnp.float32)
    y = xf * lax.rsqrt(jnp.mean(xf * xf, axis=-1, keepdims=True) + EPS)
    return (y * g.astype(jnp.float32)).astype(x.dtype)


def modulate(h, shift, scale):
    return h * (1 + scale[:, None, :]) + shift[:, None, :]


def rope_tables(positions, dtype):
    inv_freq = 1.0 / (ROPE_THETA ** (jnp.arange(0, QK_ROPE, 2, dtype=jnp.float32) / QK_ROPE))
    ang = positions.astype(jnp.float32)[..., None] * inv_freq
    return (jnp.cos(ang)[:, :, None, :].astype(dtype),
            jnp.sin(ang)[:, :, None, :].astype(dtype))


def apply_rope(x, cos, sin):
    x1, x2 = jnp.split(x, 2, axis=-1)
    return jnp.concatenate([x1 * cos - x2 * sin, x2 * cos + x1 * sin], axis=-1)


def gmlp_branch(uv, g_v, w_s, b_s):
    z = jax.nn.gelu(uv, approximate=False)
    u, v = z[..., :GM_WIDTH], z[..., GM_WIDTH:]
    v = rms_norm(v, g_v)
    b, s, _ = v.shape
    v = v.reshape(b, s // CHUNK, CHUNK, GM_GROUPS, GM_GROUP_W)
    causal = jnp.tril(jnp.ones((CHUNK, CHUNK), dtype=bool))
    w = jnp.where(causal[None], w_s, 0.0)
    mixed = jnp.einsum('gts,bnsgc->bntgc', w, v) + b_s.T[None, None, :, :, None]
    return u * mixed.reshape(b, s, GM_WIDTH)


def mla_branch(q_lat, kv_lat, k_pe, cos, sin, g_q_lat, g_kv_lat, w_uq, w_ukv, g_qn, g_kn):
    b, s, _ = q_lat.shape
    q = (rms_norm(q_lat, g_q_lat) @ w_uq).reshape(b, s, MLA_HEADS, QK_HEAD)
    kv = (rms_norm(kv_lat, g_kv_lat) @ w_ukv).reshape(b, s, MLA_HEADS, QK_NOPE + V_HEAD)
    k_nope, v = kv[..., :QK_NOPE], kv[..., QK_NOPE:]
    k = jnp.concatenate([k_nope, jnp.broadcast_to(k_pe[:, :, None, :], (b, s, MLA_HEADS, QK_ROPE))], axis=-1)
    q = rms_norm(q, g_qn)
    k = rms_norm(k, g_kn)
    q = jnp.concatenate([q[..., :QK_NOPE], apply_rope(q[..., QK_NOPE:], cos, sin)], axis=-1)
    k = jnp.concatenate([k[..., :QK_NOPE], apply_rope(k[..., QK_NOPE:], cos, sin)], axis=-1)
    q = q.transpose(0, 2, 1, 3)
    k = k.transpose(0, 2, 1, 3)
    v = v.transpose(0, 2, 1, 3)
    nb = s // Q_BLOCK
    q_blocks = q.reshape(b, MLA_HEADS, nb, Q_BLOCK, QK_HEAD).transpose(2, 0, 1, 3, 4)
    key_idx = jnp.arange(s)
    scale = QK_HEAD ** -0.5

    def attend(args):
        q_blk, i = args
        q_idx = i * Q_BLOCK + jnp.arange(Q_BLOCK)
        scores = jnp.einsum('bhqd,bhkd->bhqk', q_blk, k).astype(jnp.float32) * scale
        scores = jnp.where(key_idx[None, :] <= q_idx[:, None], scores, -jnp.inf)
        p = jax.nn.softmax(scores, axis=-1).astype(v.dtype)
        return jnp.einsum('bhqk,bhkd->bhqd', p, v)

    out = lax.map(attend, (q_blocks, jnp.arange(nb)))
    return out.transpose(1, 0, 3, 2, 4).reshape(b, s, MLA_HEADS * V_HEAD)


def setup_inputs(seed: int = 0) -> dict:
    key = jax.random.key(seed)
    ks = jax.random.split(key, 24)
    f32 = jnp.float32

    def nrm(k, shape, fan_in):
        return jax.random.normal(k, shape, f32) * (fan_in ** -0.5)

    def gain(k, shape):
        return 1.0 + 0.02 * jax.random.normal(k, shape, f32)

    L = DEPTH
    x = jax.random.normal(ks[0], (BATCH, SEQ, D_MODEL), f32)
    c = jax.random.normal(ks[1], (BATCH, D_MODEL), f32)
    positions = (jnp.arange(SEQ, dtype=jnp.int32)[None, :]
                 + jax.random.randint(ks[2], (BATCH, 1), 0, 4096, dtype=jnp.int32))
    return {
        'x': x,
        'c': c,
        'positions': positions,
        'w_ada': nrm(ks[3], (L, D_MODEL, N_MOD * D_MODEL), D_MODEL),
        'b_ada': 0.02 * jax.random.normal(ks[4], (L, N_MOD * D_MODEL), f32),
        'g_norm1': gain(ks[5], (L, D_MODEL)),
        'w_in': nrm(ks[6], (L, D_MODEL, IN_COLS), D_MODEL),
        'g_v': gain(ks[7], (L, GM_WIDTH)),
        'w_s': nrm(ks[8], (L, GM_GROUPS, CHUNK, CHUNK), CHUNK),
        'b_s': 1.0 + 0.1 * jax.random.normal(ks[9], (L, GM_GROUPS, CHUNK), f32),
        'g_q_lat': gain(ks[10], (L, Q_LORA)),
        'g_kv_lat': gain(ks[11], (L, KV_LORA)),
        'w_uq': nrm(ks[12], (L, Q_LORA, MLA_HEADS * QK_HEAD), Q_LORA),
        'w_ukv': nrm(ks[13], (L, KV_LORA, MLA_HEADS * (QK_NOPE + V_HEAD)), KV_LORA),
        'g_qn': gain(ks[14], (L, QK_HEAD)),
        'g_kn': gain(ks[15], (L, QK_HEAD)),
        'w_branch_a': nrm(ks[16], (L, GM_WIDTH, D_MODEL), GM_WIDTH),
        'w_branch_b': nrm(ks[17], (L, MLA_HEADS * V_HEAD, D_MODEL), MLA_HEADS * V_HEAD),
        'w_out': nrm(ks[18], (L, D_MODEL, D_MODEL), D_MODEL),
        'g_norm2': gain(ks[19], (L, D_MODEL)),
        'w_ff1': nrm(ks[20], (L, D_MODEL, D_FF), D_MODEL),
        'w_ff2': nrm(ks[21], (L, D_FF, D_MODEL), D_FF),
    }


def reference(x, c, positions, w_ada, b_ada, g_norm1, w_in, g_v, w_s, b_s, g_q_lat, g_kv_lat,
              w_uq, w_ukv, g_qn, g_kn, w_branch_a, w_branch_b, w_out, g_norm2, w_ff1, w_ff2):
    cos, sin = rope_tables(positions, x.dtype)
    cond = jax.nn.silu(c)
    for l in range(DEPTH):
        mod = cond @ w_ada[l] + b_ada[l]
        sh1, sc1, ga1, sh2, sc2, ga2 = jnp.split(mod, N_MOD, axis=-1)
        h = modulate(rms_norm(x, g_norm1[l]), sh1, sc1)
        proj = h @ w_in[l]
        y_a = gmlp_branch(proj[..., :OFF_Q], g_v[l], w_s[l], b_s[l]) @ w_branch_a[l]
        y_b = mla_branch(proj[..., OFF_Q:OFF_KV], proj[..., OFF_KV:OFF_KPE], proj[..., OFF_KPE:OFF_GATE],
                         cos, sin, g_q_lat[l], g_kv_lat[l], w_uq[l], w_ukv[l], g_qn[l], g_kn[l]) @ w_branch_b[l]
        gate_a = jax.nn.sigmoid(proj[..., OFF_GATE:OFF_GATE + D_MODEL])
        gate_b = jax.nn.sigmoid(proj[..., OFF_GATE + D_MODEL:])
        mixed = gate_a * y_a + gate_b * y_b
        x = x + ga1[:, None, :] * (mixed @ w_out[l])
        h = modulate(rms_norm(x, g_norm2[l]), sh2, sc2)
        ff = jnp.square(jax.nn.relu(h @ w_ff1[l])) @ w_ff2[l]
        x = x + ga2[:, None, :] * ff
    return x
```

```python
import os
import contextlib
import numpy as np
import concourse.bass as bass
import concourse.mybir as mybir
from concourse.bass_utils import run_bass_kernel_spmd

F32 = mybir.dt.float32
BF16 = mybir.dt.bfloat16
I32 = mybir.dt.int32
AF = mybir.ActivationFunctionType
OP = mybir.AluOpType
AX = mybir.AxisListType

D = 4096
S = 2048
NB = 4
GMW = 2048
HEADS = 32
QL = 1024
KVL = 512
DFF = 16384
OFF_Q = 4096
OFF_KV = OFF_Q + QL
OFF_KPE = OFF_KV + KVL
OFF_GATE = OFF_KPE + 64
EPS = 1e-6
SCALE = 192.0 ** -0.5
BL = [[0, 3, 4, 7, 8, 11, 12, 15], [1, 2, 5, 6, 9, 10, 13, 14]]
NRING = 4
G = 4
TWO_PI = 2.0 * np.pi
C1 = 6.28125
C2 = float(TWO_PI - C1)
MAGIC = 12582912.0

SM = {}
_o = 0
for _n, _w in [("cT", 32), ("b_ada", 192), ("g1", 32), ("g2", 32), ("gql", 8), ("gkvl", 4), ("gv", 16),
               ("gqn", 3), ("gkn", 3), ("invf", 1), ("sgn", 1)]:
    SM[_n] = (_o, _w)
    _o += _w
NSM = _o

DEBUG = os.environ.get("MK_DEBUG", "")


class Ctx:
    def __init__(self, nc, es):
        self.nc = nc
        self.es = es
        self.eng = {}
        for name, obj in [("pe", nc.tensor), ("act", nc.scalar), ("dve", nc.vector),
                          ("pool", nc.gpsimd), ("sp", nc.sync)]:
            sem = es.enter_context(nc.semaphore("sem_" + name))
            self.eng[name] = {"o": obj, "sem": sem, "cnt": 0, "known": {}, "name": name}
        self.w = {}
        self.r = {}
        self.dsem = {}
        self.dcnt = {}
        self.nwait = 0

    def _need(self, e, ev):
        if ev is None:
            return
        sem, val = ev
        k = id(sem)
        if e["known"].get(k, 0) >= val:
            return
        e["o"].wait_ge(sem, val)
        self.nwait += 1
        e["known"][k] = val

    def _deps(self, e, reads, writes):
        for k in reads:
            self._need(e, self.w.get(k))
            if k.startswith("ps"):
                for ev in self.r.get(k, ()):
                    if ev[0] is not e["sem"]:
                        self._need(e, ev)
        for k in writes:
            self._need(e, self.w.get(k))
            for ev in self.r.get(k, ()):
                if ev[0] is e["sem"]:
                    continue
                self._need(e, ev)

    def _commit(self, ev, reads, writes):
        for k in writes:
            self.w[k] = ev
            self.r[k] = []
        for k in reads:
            self.r.setdefault(k, []).append(ev)

    def op(self, en, emits, reads=(), writes=()):
        e = self.eng[en]
        self._deps(e, reads, writes)
        if callable(emits):
            emits = [emits]
        ins = None
        for f in emits:
            ins = f()
        e["cnt"] += 1
        ins.then_inc(e["sem"], 1)
        ev = (e["sem"], e["cnt"])
        self._commit(ev, reads, writes)

    def dma(self, q, emit, semname, reads=(), writes=()):
        e = self.eng[q]
        self._deps(e, reads, writes)
        if semname not in self.dsem:
            self.dsem[semname] = self.es.enter_context(self.nc.semaphore("d_" + semname))
            self.dcnt[semname] = 0
        ins = emit()
        self.dcnt[semname] += 16
        ins.then_inc(self.dsem[semname], 16)
        ev = (self.dsem[semname], self.dcnt[semname])
        self._commit(ev, reads, writes)

    def barrier(self, extra_dsems=()):
        evs = [(self.eng[n]["sem"], self.eng[n]["cnt"]) for n in ("pe", "act", "dve") if self.eng[n]["cnt"] > 0]
        for s in extra_dsems:
            if s in self.dsem:
                evs.append((self.dsem[s], self.dcnt[s]))
        for n in ("pe", "act", "dve", "sp"):
            for ev in evs:
                if ev[0] is self.eng[n]["sem"]:
                    continue
                self._need(self.eng[n], ev)

    def final_wait(self, q, semnames):
        for s in semnames:
            self._need(self.eng[q], (self.dsem[s], self.dcnt[s]))


NSLOT_HALF = 8 + 24 + 16 + 16 + 112 + 32 + 256


def build_program():
    nc = bass.Bass("TRN2", target_bir_lowering=False)
    dr = {}

    def din(name, shape, dt=F32):
        dr[name] = nc.dram_tensor(name, shape, dt, kind="ExternalInput").ap()

    din("xTb", [D, S]); din("xTo", [D, 1024]); din("posb", [64, S], I32); din("poso", [64, 1024], I32)
    din("smalls", [128, NSM]); din("grow", [1, 384]); din("wsT", [128, 1024]); din("tri", [128, 128])
    din("bsb", [128, 1024]); din("maskT", [128, 2048]); din("WA", [192, 128, 4096]); din("WKV", [5, 128, 4096])
    din("WS", [NSLOT_HALF, 128, 4096])
    dr["outT"] = nc.dram_tensor("outT", [D, 1024], F32, kind="ExternalOutput").ap()
    dbg = {}
    if DEBUG:
        for nm, shp, dt in [("d_mod", [128, 192], F32), ("d_kvn", [128, 4 * S], BF16), ("d_kpe", [64, S], BF16),
                            ("d_ssqpe", [128, 16], F32), ("d_negC", [128, 1], F32),
                            ("d_hT", [128, 32 * 512], BF16), ("d_qlatn", [128, 8 * 512], BF16),
                            ("d_oT", [128, 32 * 512], BF16), ("d_gm", [128, 16 * 512], BF16),
                            ("d_mixed", [128, 32 * 512], BF16), ("d_x1", [128, 32 * 512], F32),
                            ("d_h2", [128, 32 * 512], BF16)]:
            dbg[nm] = nc.dram_tensor(nm, shp, dt, kind="ExternalOutput").ap()

    with contextlib.ExitStack() as es:
        cx = Ctx(nc, es)
        E = cx.eng
        pe, act, dve, pool, sp = (E[n]["o"] for n in ("pe", "act", "dve", "pool", "sp"))

        def sb(name, shape, dt):
            return es.enter_context(nc.sbuf_tensor("s_" + name, shape, dt))

        ps = [es.enter_context(nc.psum_tensor(f"ps{i}", [128, 512], F32)) for i in range(8)]
        PS = [f"ps{i}" for i in range(8)]
        ring = [sb(f"ring{i}", [128, 4096], BF16) for i in range(NRING)]
        RA = sb("RA", [128, 16384], F32)
        RB = sb("RB", [128, 16384], BF16)
        RC = sb("RC", [128, 16384], BF16)
        smalls = sb("smalls", [128, NSM], F32)
        mod = sb("mod", [128, 192], F32)
        cond_bf = sb("cond_bf", [128, 32], BF16)
        mrow_acc = sb("mrow_acc", [1, 512], F32)
        a1 = sb("a1", [128, 32], F32)
        a2 = sb("a2", [128, 32], F32)
        ones_bf = sb("ones_bf", [128, 128], BF16)
        ones_f = sb("ones_f", [128, 128], F32)
        negC = sb("negC", [128, 1], F32)
        kvn = sb("kvn", [128, 4, S], BF16)
        kpe_rot = sb("kpe_rot", [64, S], BF16)
        ssq_pe = sb("ssq_pe", [128, 16], F32)
        Cq = sb("Cq", [64, 1024], F32)
        Sq = sb("Sq", [64, 1024], F32)
        maskb = sb("maskb", [128, 2048], BF16)
        wsm = sb("wsm", [128, 1024], BF16)
        bsb = sb("bsb", [128, 1024], F32)
        regs = {"A": (RA, 4), "B": (RB, 2), "C": (RC, 2)}

        def view(reg, off, dt, dims):
            base, esz = regs[reg]
            dsz = 4 if dt in (F32, I32) else 2
            n = 1
            for d_ in dims:
                n *= d_
            nb = n * dsz
            assert off % 4 == 0 and off + nb <= 16384 * esz, (reg, off, nb)
            v = base[:, off // esz:(off + nb) // esz]
            bdt = F32 if reg == "A" else BF16
            if dt != bdt:
                v = v.bitcast(dt)
            if len(dims) == 2:
                v = v.rearrange("p (a b) -> p a b", b=dims[1])
            return v

        def smc(name, j=0, w=1, rows=128):
            o, _ = SM[name]
            return smalls[0:rows, o + j:o + j + w]

        rstate = {"issued": 0}

        def ring_issue(src_ap):
            i = rstate["issued"] % NRING
            rstate["issued"] += 1
            t = ring[i]
            cx.dma("pool", lambda: pool.dma_start(out=t[:, :].rearrange("p (a b) -> p a b", b=2048),
                                                  in_=src_ap.rearrange("p (a b) -> p a b", b=2048)),
                   f"ring{i}", writes=[f"ring{i}"])
            return i

        class Stream:
            def __init__(self, srcs):
                self.srcs = srcs
                self.nxt = 0
                self.q = []

            def fill(self):
                while self.nxt < len(self.srcs) and len(self.q) < NRING - 1:
                    self.q.append(ring_issue(self.srcs[self.nxt]))
                    self.nxt += 1

            def get(self):
                self.fill()
                i = self.q.pop(0)
                self.fill()
                return ring[i], f"ring{i}"

        def dbg_out(name, src, key):
            if DEBUG:
                cx.dma("sp", lambda: sp.dma_start(out=dbg[name], in_=src), "dbg", reads=key if isinstance(key, list) else [key])

        def rsqrt_act(dst, dkey, src, skey, scale):
            cx.op("act", lambda: act.activation(out=dst, in_=src, func=AF.Ln, scale=scale, bias=EPS),
                  reads=[skey], writes=[dkey])
            cx.op("act", lambda: act.activation(out=dst, in_=dst, func=AF.Exp, scale=-0.5), reads=[dkey], writes=[dkey])

        cx.dma("sp", lambda: sp.dma_start(out=smalls[:, :], in_=dr["smalls"]), "c_sm", writes=["smalls"])
        cx.dma("sp", lambda: sp.dma_start(out=bsb[:, :], in_=dr["bsb"]), "c_bsb", writes=["bsb"])
        cx.op("dve", lambda: dve.memset(ones_bf[:, :], 1.0), writes=["ones_bf"])
        cx.op("dve", lambda: dve.memset(ones_f[:, :], 1.0), writes=["ones_f"])

        tmpA = view("B", 0, F32, [2048])
        tmpB = view("B", 8192, F32, [1024])
        tri = view("B", 12288, F32, [128])
        grow = view("B", 12800, F32, [384])
        gm2 = view("B", 14336, F32, [4])
        cx.dma("sp", lambda: sp.dma_start(out=tmpA, in_=dr["maskT"]), "c_a", writes=["tmpA"])
        cx.dma("sp", lambda: sp.dma_start(out=tmpB, in_=dr["wsT"]), "c_b", writes=["tmpB"])
        cx.dma("sp", lambda: sp.dma_start(out=tri, in_=dr["tri"]), "c_c", writes=["tri"])
        cx.dma("sp", lambda: sp.dma_start(out=grow[0:1, :], in_=dr["grow"]), "c_d", writes=["grow"])
        cx.op("dve", lambda: dve.tensor_copy(out=maskb[:, :], in_=tmpA), reads=["tmpA"], writes=["maskb"])
        for g in range(8):
            cx.op("dve", lambda g=g: dve.tensor_tensor(out=wsm[:, g * 128:(g + 1) * 128], in0=tmpB[:, g * 128:(g + 1) * 128],
                                                       in1=tri, op=OP.mult), reads=["tmpB", "tri"], writes=["wsm"])
        cx.op("dve", lambda: dve.tensor_reduce(out=gm2[0:1, 0:1], in_=grow[0:1, 0:192], axis=AX.X, op=OP.max,
                                               apply_absolute_value=True), reads=["grow"], writes=["gm2a"])
        cx.op("dve", lambda: dve.tensor_reduce(out=gm2[0:1, 1:2], in_=grow[0:1, 192:384], axis=AX.X, op=OP.max,
                                               apply_absolute_value=True), reads=["grow"], writes=["gm2b"])
        cx.op("dve", lambda: dve.tensor_tensor(out=gm2[0:1, 2:3], in0=gm2[0:1, 0:1], in1=gm2[0:1, 1:2], op=OP.mult),
              reads=["gm2a", "gm2b"], writes=["gm2c"])
        cx.op("dve", lambda: dve.tensor_scalar(out=gm2[0:1, 3:4], in0=gm2[0:1, 2:3], scalar1=-(192.0 ** 0.5),
                                               scalar2=None, op0=OP.mult), reads=["gm2c"], writes=["gm2d"])
        cx.op("pe", lambda: pe.matmul(ps[0][:, 0:1], lhsT=ones_f[0:1, :], rhs=gm2[0:1, 3:4], start=True, stop=True),
              reads=["ones_f", "gm2d"], writes=[PS[0]])
        cx.op("dve", lambda: dve.tensor_copy(out=negC[:, :], in_=ps[0][:, 0:1]), reads=[PS[0]], writes=["negC"])
        dbg_out("d_negC", negC[:, :], "negC")

        Ck = view("C", 16384, BF16, [S])[0:64, :]
        Sk = view("C", 20480, BF16, [S])[0:64, :]
        sq4 = [view("C", 24576, BF16, [4, 512]), view("C", 28672, BF16, [4, 512])]

        def rope_tables(pos_ap, T, gname, Ct, St, ckey, skey, pfx, toff):
            posi = view("A", toff, I32, [T])[0:64, :]
            ang = view("A", toff + 4 * T, F32, [T])[0:64, :]
            t1 = view("A", toff + 8 * T, F32, [T])[0:64, :]
            t2 = view("A", toff + 12 * T, F32, [T])[0:64, :]
            kk = pfx
            cx.dma("sp", lambda: sp.dma_start(out=posi, in_=pos_ap), "c_" + pfx, writes=[kk + "posi"])
            cx.op("dve", lambda: dve.tensor_copy(out=ang, in_=posi), reads=[kk + "posi"], writes=[kk + "ang"])
            cx.op("dve", lambda: dve.tensor_scalar(out=ang, in0=ang, scalar1=smc("invf", rows=64), scalar2=None, op0=OP.mult),
                  reads=[kk + "ang", "smalls"], writes=[kk + "ang"])

            def reduce_sin(dst, dstkey):
                cx.op("dve", lambda: dve.tensor_scalar(out=t1, in0=ang, scalar1=1.0 / TWO_PI, scalar2=MAGIC,
                                                       op0=OP.mult, op1=OP.add), reads=[kk + "ang"], writes=[kk + "t1"])
                cx.op("dve", lambda: dve.tensor_scalar(out=t1, in0=t1, scalar1=-MAGIC, scalar2=None, op0=OP.add),
                      reads=[kk + "t1"], writes=[kk + "t1"])
                cx.op("dve", lambda: dve.scalar_tensor_tensor(out=t2, in0=t1, scalar=-C1, in1=ang, op0=OP.mult, op1=OP.add),
                      reads=[kk + "t1", kk + "ang"], writes=[kk + "t2"])
                cx.op("dve", lambda: dve.scalar_tensor_tensor(out=t2, in0=t1, scalar=-C2, in1=t2, op0=OP.mult, op1=OP.add),
                      reads=[kk + "t1", kk + "t2"], writes=[kk + "t2"])
                cx.op("dve", lambda: dve.tensor_scalar(out=t2, in0=t2, scalar1=-3.14159, scalar2=3.14159, op0=OP.max, op1=OP.min),
                      reads=[kk + "t2"], writes=[kk + "t2"])
                cx.op("act", lambda: act.activation(out=dst, in_=t2, func=AF.Sin), reads=[kk + "t2"], writes=[dstkey])

            reduce_sin(St, skey)
            cx.op("dve", lambda: dve.tensor_scalar(out=ang, in0=ang, scalar1=float(np.pi / 2), scalar2=None, op0=OP.add),
                  reads=[kk + "ang"], writes=[kk + "ang"])
            reduce_sin(Ct, ckey)
            o, _ = SM[gname]
            cx.op("dve", lambda: dve.tensor_scalar(out=Ct, in0=Ct, scalar1=smalls[0:64, o + 1:o + 2], scalar2=None, op0=OP.mult),
                  reads=[ckey, "smalls"], writes=[ckey])
            cx.op("dve", lambda: dve.tensor_scalar(out=St, in0=St, scalar1=smalls[0:64, o + 2:o + 3], scalar2=smc("sgn", rows=64),
                                                   op0=OP.mult, op1=OP.mult), reads=[skey, "smalls"], writes=[skey])

        rope_rec = []
        _op, _dma = cx.op, cx.dma
        cx.op = lambda *a, **k: rope_rec.append((_op, a, k))
        cx.dma = lambda *a, **k: rope_rec.append((_dma, a, k))
        rope_tables(dr["posb"], S, "gkn", Ck, Sk, "Ck", "Sk", "rk_", 0)
        rope_tables(dr["poso"], 1024, "gqn", Cq[:, :], Sq[:, :], "Cq", "Sq", "rq_", 32768)
        cx.op, cx.dma = _op, _dma

        o_c, _ = SM["cT"]
        o_b, _ = SM["b_ada"]
        cx.op("act", lambda: act.activation(out=cond_bf[:, :], in_=smalls[:, o_c:o_c + 32], func=AF.Silu),
              reads=["smalls"], writes=["cond"])
        WAd = dr["WA"]

        class BG:
            nxt = 64

        def bg_srcs(n):
            out = [WAd[BG.nxt + i] for i in range(n) if BG.nxt + i < 192]
            return out

        def ada_item(strm_, si, pbank):
            ct, kq = si // 4, si % 4
            wt, wk = strm_.get()
            cx.op("pe", [lambda kk=kk: pe.matmul(ps[pbank][0:1, :], lhsT=cond_bf[:, kq * 8 + kk:kq * 8 + kk + 1],
                                                 rhs=wt[:, kk * 512:(kk + 1) * 512], start=(kk == 0), stop=(kk == 7))
                         for kk in range(8)], reads=[wk, "cond"], writes=[PS[pbank]])
            if kq == 0:
                cx.op("act", lambda: act.copy(out=mrow_acc[0:1, :], in_=ps[pbank][0:1, :]), reads=[PS[pbank]], writes=["mrow"])
            else:
                cx.op("dve", lambda: dve.tensor_tensor(out=mrow_acc[0:1, :], in0=ps[pbank][0:1, :], in1=mrow_acc[0:1, :], op=OP.add),
                      reads=[PS[pbank], "mrow"], writes=["mrow"])
            if kq == 3:
                cx.op("pe", [lambda j=j: pe.matmul(ps[pbank][:, j:j + 1], lhsT=mrow_acc[0:1, j * 128:(j + 1) * 128],
                                                   rhs=ones_f[0:1, 0:1], start=True, stop=True) for j in range(4)],
                      reads=["mrow", "ones_f"], writes=[PS[pbank]])
                mk = "mod_a" if ct < 16 else "mod_b"
                cx.op("dve", lambda: dve.tensor_tensor(out=mod[:, ct * 4:ct * 4 + 4], in0=ps[pbank][:, 0:4],
                                                       in1=smalls[:, o_b + ct * 4:o_b + ct * 4 + 4], op=OP.add),
                      reads=[PS[pbank], "smalls"], writes=[mk])

        def bg_work(strm_, n, pbank):
            for _ in range(n):
                if BG.nxt >= 192:
                    return
                ada_item(strm_, BG.nxt, pbank)
                BG.nxt += 1
                if BG.nxt == 192:
                    og2, _ = SM["g2"]
                    cx.op("dve", lambda: dve.scalar_tensor_tensor(out=a2[:, :], in0=mod[:, 128:160], scalar=1.0,
                                                                  in1=smalls[:, og2:og2 + 32], op0=OP.add, op1=OP.mult),
                          reads=["mod_b", "smalls"], writes=["a2"])
                    dbg_out("d_mod", mod[:, :], "mod_b")

        astrm = Stream([WAd[si] for si in range(64)])
        astrm.fill()
        for si in range(64):
            ada_item(astrm, si, 1 + (si // 4) % 2)
            if rope_rec:
                f_, a_, k_ = rope_rec.pop(0)
                f_(*a_, **k_)
        while rope_rec:
            f_, a_, k_ = rope_rec.pop(0)
            f_(*a_, **k_)
        og1, _ = SM["g1"]
        cx.op("dve", lambda: dve.scalar_tensor_tensor(out=a1[:, :], in0=mod[:, 32:64], scalar=1.0, in1=smalls[:, og1:og1 + 32],
                                                      op0=OP.add, op1=OP.mult), reads=["mod_a", "smalls"], writes=["a1"])
        cx.barrier(["dbg"])

        RAK = ["ra0", "ra1", "ra2", "ra3"]

        def norm_stats(x3, xkey, sq, sqkeys, psb, nch, hook=None):
            ng = nch // 4
            for g in range(ng):
                cx.op("act", lambda g=g: act.activation(out=sq4[g % 2], in_=x3[:, 4 * g:4 * g + 4, :], func=AF.Square),
                      reads=[xkey[g // 2]], writes=[f"sq4_{g % 2}"])
                cx.op("pe", [lambda g=g, j=j: pe.matmul(ps[psb][:, :], lhsT=ones_bf[:, :], rhs=sq4[g % 2][:, j, :],
                                                        start=(g == 0 and j == 0), stop=(g == ng - 1 and j == 3)) for j in range(4)],
                      reads=[f"sq4_{g % 2}", "ones_bf"], writes=[PS[psb]])
                if hook:
                    hook()

        def norm_apply(x3, xkey, h3, hkey, avec, akey, shcol0, rstd, rkey, tmp, tkeys, hook=None):
            for k in range(32):
                cx.op("dve", lambda k=k: dve.scalar_tensor_tensor(out=tmp[k % 2], in0=x3[:, k, :], scalar=avec[:, k:k + 1],
                                                                  in1=rstd, op0=OP.mult, op1=OP.mult),
                      reads=[xkey[k // 8], akey, rkey], writes=[tkeys[k % 2]])
                cx.op("act", lambda k=k: act.activation(out=h3[:, k, :], in_=tmp[k % 2], func=AF.Identity,
                                                        bias=mod[:, shcol0 + k:shcol0 + k + 1]),
                      reads=[tkeys[k % 2], "mod_a" if shcol0 == 0 else "mod_b"], writes=[hkey])
                if hook and k % 4 == 3:
                    hook()

        xTa = view("A", 0, F32, [32, 512])
        hTb = view("B", 0, BF16, [32, 512])
        sqv = [view("C", 0, BF16, [512]), view("C", 1024, BF16, [512])]
        tmpv = [view("C", 2048, F32, [512]), view("C", 4096, F32, [512])]
        rstdv = view("C", 6144, F32, [512])
        kpr = view("C", 8192, F32, [512])[0:64, :]
        kpsw = view("C", 10240, F32, [512])[0:64, :]
        sqp = view("C", 12288, BF16, [512])[0:64, :]
        xvb = dr["xTb"].rearrange("(k p) t -> p k t", p=128)
        ksrcs = []
        _n = 64
        for _ in range(4):
            for _j in range(10):
                if _n < 192:
                    ksrcs.append(WAd[_n]); _n += 1
            for i in range(5):
                ksrcs.append(dr["WKV"][i])
        kstrm = Stream(ksrcs)
        kstrm.fill()
        for tt in range(4):
            for kq in range(4):
                cx.dma("sp", lambda tt=tt, kq=kq: sp.dma_start(out=xTa[:, kq * 8:(kq + 1) * 8, :],
                                                               in_=xvb[:, kq * 8:(kq + 1) * 8, tt * 512:(tt + 1) * 512]),
                       f"xT{kq}", writes=[RAK[kq]])
            kcalls = [0]

            def khook():
                kcalls[0] += 1
                if kcalls[0] in (1, 3, 5, 6, 8, 10, 11, 13, 15, 16):
                    bg_work(kstrm, 1, 6 + BG.nxt % 2)
            norm_stats(xTa, RAK, sqv, ["sq0", "sq1"], 0, 32, hook=khook)
            rsqrt_act(rstdv, "rstd", ps[0][:, :], PS[0], 1.0 / D)
            norm_apply(xTa, RAK, hTb, "hT", a1, "a1", 0, rstdv, "rstd", tmpv, ["tmp0", "tmp1"], hook=khook)
            for ft in range(4):
                wt, wk = kstrm.get()
                cx.op("pe", [lambda ft=ft, k=k, wt=wt: pe.matmul(ps[1 + ft][:, :], lhsT=wt[:, k * 128:(k + 1) * 128],
                                                                 rhs=hTb[:, k, :], start=(k == 0), stop=(k == 31))
                             for k in range(32)], reads=[wk, "hT"], writes=[PS[1 + ft]])
                cx.op("act", lambda ft=ft: act.activation(out=sqv[ft % 2], in_=ps[1 + ft][:, :], func=AF.Square),
                      reads=[PS[1 + ft]], writes=[f"sq{ft % 2}"])
                cx.op("pe", lambda ft=ft: pe.matmul(ps[5][:, :], lhsT=ones_bf[:, :], rhs=sqv[ft % 2], start=(ft == 0), stop=(ft == 3)),
                      reads=[f"sq{ft % 2}", "ones_bf"], writes=[PS[5]])
            rsqrt_act(rstdv, "rstd", ps[5][:, :], PS[5], 1.0 / KVL)
            for ft in range(4):
                cx.op("dve", lambda ft=ft, tt=tt: dve.scalar_tensor_tensor(
                    out=kvn[:, ft, tt * 512:(tt + 1) * 512], in0=ps[1 + ft][:, :], scalar=smc("gkvl", ft), in1=rstdv,
                    op0=OP.mult, op1=OP.mult), reads=[PS[1 + ft], "rstd", "smalls"], writes=["kvn"])
            wt, wk = kstrm.get()
            cx.op("pe", [lambda k=k, wt=wt: pe.matmul(ps[6][0:64, :], lhsT=wt[:, k * 128:k * 128 + 64], rhs=hTb[:, k, :],
                                                      start=(k == 0), stop=(k == 31)) for k in range(32)],
                  reads=[wk, "hT"], writes=[PS[6]])
            cx.op("pe", [lambda k=k, wt=wt: pe.matmul(ps[7][0:64, :], lhsT=wt[:, k * 128 + 64:k * 128 + 128], rhs=hTb[:, k, :],
                                                      start=(k == 0), stop=(k == 31)) for k in range(32)],
                  reads=[wk, "hT"], writes=[PS[7]])
            cx.op("act", lambda: act.activation(out=sqp, in_=ps[6][0:64, :], func=AF.Square), reads=[PS[6]], writes=["sqp"])
            cx.op("pe", [lambda j=j, tt=tt: pe.matmul(ps[0][:, tt * 4 + j:tt * 4 + j + 1], lhsT=sqp[:, j * 128:(j + 1) * 128],
                                                      rhs=ones_bf[0:64, 0:1], start=True, stop=True) for j in range(4)],
                  reads=["sqp", "ones_bf"], writes=[PS[0]])
            cx.op("dve", lambda tt=tt: dve.tensor_copy(out=ssq_pe[:, tt * 4:tt * 4 + 4], in_=ps[0][:, tt * 4:tt * 4 + 4]),
                  reads=[PS[0]], writes=["ssq_pe"])
            cx.op("dve", lambda tt=tt: dve.tensor_tensor(out=kpr, in0=ps[6][0:64, :], in1=Ck[:, tt * 512:(tt + 1) * 512], op=OP.mult),
                  reads=[PS[6], "Ck"], writes=["kpr"])
            cx.op("dve", lambda tt=tt: dve.tensor_tensor(out=kpsw, in0=ps[7][0:64, :], in1=Sk[:, tt * 512:(tt + 1) * 512], op=OP.mult),
                  reads=[PS[7], "Sk"], writes=["kpsw"])
            cx.op("dve", lambda tt=tt: dve.tensor_tensor(out=kpe_rot[:, tt * 512:(tt + 1) * 512], in0=kpr, in1=kpsw, op=OP.add),
                  reads=["kpr", "kpsw"], writes=["kpe_rot"])
        dbg_out("d_kvn", kvn[:, :, :].rearrange("p a b -> p (a b)"), "kvn")
        dbg_out("d_kpe", kpe_rot[:, :], "kpe_rot")
        dbg_out("d_ssqpe", ssq_pe[:, :], "ssq_pe")
        cx.barrier(["dbg"])

        WS = dr["WS"]
        xo = dr["xTo"].rearrange("(k p) t -> p k t", p=128)
        ov = dr["outT"].rearrange("(k p) t -> p k t", p=128)
        gq0, _ = SM["gqn"]
        gk0, _ = SM["gkn"]
        nhalf = int(os.environ.get("MK_NHALF", "2"))
        stage = int(os.environ.get("MK_STAGE", "9"))

        for hf in range(nhalf):
            hsrcs = []
            _n = BG.nxt
            for _j in range(8):
                if _n < 192:
                    hsrcs.append(WAd[_n]); _n += 1
            for i in range(NSLOT_HALF):
                hsrcs.append(WS[i])
                if i < 8:
                    nb = 1
                elif i < 32:
                    nb = 3
                else:
                    nb = 0
                for _j in range(nb):
                    if _n < 192:
                        hsrcs.append(WAd[_n]); _n += 1
            strm = Stream(hsrcs)
            strm.fill()
            NKB = 8 * (hf + 1)
            NKT = NKB // 4
            tok0 = hf * 512
            d0 = (hf == 0)
            xT = view("A", 0, F32, [32, 512])
            hT = view("B", 0, BF16, [32, 512])
            oT = view("A", 0, BF16, [32, 512])
            gmT = view("A", 32768, BF16, [16, 512])
            qlatn = view("A", 49152, BF16, [8, 512])
            mixed = view("C", 0, BF16, [32, 512])
            acc = view("A", 0, F32, [32, 512])
            h2T = view("B", 0, BF16, [32, 512])

            if hf == 0:
                for kq in range(4):
                    cx.dma("sp", lambda kq=kq: sp.dma_start(out=xT[:, kq * 8:(kq + 1) * 8, :], in_=xo[:, kq * 8:(kq + 1) * 8, tok0:tok0 + 512]),
                           f"xT{kq}", writes=[RAK[kq]])
            hcalls = [0]

            def hhook():
                hcalls[0] += 1
                if hcalls[0] % 2 == 0:
                    bg_work(strm, 1, 6 + BG.nxt % 2)
            norm_stats(xT, RAK, sqv, ["sq0", "sq1"], 0, 32, hook=hhook)
            rsqrt_act(rstdv, "rstd", ps[0][:, :], PS[0], 1.0 / D)
            norm_apply(xT, RAK, hT, "hT", a1, "a1", 0, rstdv, "rstd", tmpv, ["tmp0", "tmp1"], hook=hhook)
            if d0:
                dbg_out("d_hT", hT.rearrange("p a b -> p (a b)"), "hT")
            cx.barrier(["dbg"])
            if stage < 2:
                continue

            qraw = view("C", 8192, F32, [8, 512])
            for ft in range(8):
                wt, wk = strm.get()
                b = 1 + ft % 2
                cx.op("pe", [lambda k=k, wt=wt, b=b: pe.matmul(ps[b][:, :], lhsT=wt[:, k * 128:(k + 1) * 128], rhs=hT[:, k, :],
                                                               start=(k == 0), stop=(k == 31)) for k in range(32)],
                      reads=[wk, "hT"], writes=[PS[b]])
                cx.op("act", lambda ft=ft, b=b: act.copy(out=qraw[:, ft, :], in_=ps[b][:, :]), reads=[PS[b]], writes=["qraw"])
                cx.op("act", lambda ft=ft, b=b: act.activation(out=sqv[ft % 2], in_=ps[b][:, :], func=AF.Square),
                      reads=[PS[b]], writes=[f"sq{ft % 2}"])
                cx.op("pe", lambda ft=ft: pe.matmul(ps[3][:, :], lhsT=ones_bf[:, :], rhs=sqv[ft % 2], start=(ft == 0), stop=(ft == 7)),
                      reads=[f"sq{ft % 2}", "ones_bf"], writes=[PS[3]])
                bg_work(strm, 1, 7)
            rsqrt_act(rstdv, "rstd", ps[3][:, :], PS[3], 1.0 / QL)
            for ft in range(8):
                cx.op("dve", lambda ft=ft: dve.scalar_tensor_tensor(out=qlatn[:, ft, :], in0=qraw[:, ft, :], scalar=smc("gql", ft),
                                                                    in1=rstdv, op0=OP.mult, op1=OP.mult),
                      reads=["qraw", "rstd", "smalls"], writes=["qlatn"])
            if d0:
                dbg_out("d_qlatn", qlatn.rearrange("p a b -> p (a b)"), "qlatn")
            cx.barrier(["dbg"])
            if stage < 3:
                continue

            kT = view("C", 0, BF16, [G, NKB * 128])
            vtok = view("C", 16384, BF16, [NKB, G * 128])
            o3 = 32768
            qTn = [view("A", o3 + i * 1024, BF16, [512]) for i in range(2)]
            qTr = [view("A", o3 + 2048 + i * 1024, BF16, [512])[0:64, :] for i in range(2)]
            sqk = [view("A", o3 + 4096 + i * 1024, BF16, [512]) for i in range(2)]
            sqr = view("A", o3 + 6144, BF16, [512])[0:64, :]
            pT = [view("A", o3 + 7168 + i * 1024, BF16, [512]) for i in range(3)]
            rks = view("A", o3 + 10240, F32, [G * 16])
            rq = view("A", 57344, F32, [512])
            r1 = view("A", 59392, F32, [512])[0:64, :]
            r2 = view("A", 61440, F32, [512])[0:64, :]
            rin = view("A", 63488, F32, [512])
            pTB = [view("A", o3 + 10752 + i * 1024, BF16, [512]) for i in range(3)]
            for hg in range(HEADS // G):
                wkvt, wkvk = strm.get()
                ktiles = [(hl_, kt_) for hl_ in range(G) for kt_ in range(NKT)]

                def k_mm(i):
                    hl_, kt_ = ktiles[i]
                    b = i % 2
                    cx.op("pe", [lambda k=k: pe.matmul(
                        ps[b][:, :], lhsT=wkvt[:, hl_ * 512 + k * 128:hl_ * 512 + (k + 1) * 128],
                        rhs=kvn[:, k, kt_ * 512:(kt_ + 1) * 512], start=(k == 0), stop=(k == 3)) for k in range(4)],
                        reads=[wkvk, "kvn"], writes=[PS[b]])
                    cx.op("dve", lambda: dve.tensor_scalar(out=kT[:, hl_, kt_ * 512:(kt_ + 1) * 512], in0=ps[b][:, :],
                                                           scalar1=smalls[:, gk0:gk0 + 1], scalar2=None, op0=OP.mult),
                          reads=[PS[b], "smalls"], writes=["kT"])
                    cx.op("act", lambda: act.activation(out=sqk[b], in_=ps[b][:, :], func=AF.Square),
                          reads=[PS[b]], writes=[f"sqk{b}"])

                def k_stats(i):
                    hl_, kt_ = ktiles[i]
                    b = i % 2
                    cx.op("pe", [lambda j=j: pe.matmul(
                        ps[2][:, hl_ * 16 + kt_ * 4 + j:hl_ * 16 + kt_ * 4 + j + 1], lhsT=sqk[b][:, j * 128:(j + 1) * 128],
                        rhs=ones_bf[:, 0:1], start=True, stop=True) for j in range(4)],
                        reads=[f"sqk{b}", "ones_bf"], writes=[PS[2]])

                for i in range(len(ktiles)):
                    k_mm(i)
                    if i > 0:
                        k_stats(i - 1)
                k_stats(len(ktiles) - 1)
                for hl in range(G):
                    cx.op("dve", lambda hl=hl: dve.tensor_tensor(out=rks[:, hl * 16:hl * 16 + NKB], in0=ps[2][:, hl * 16:hl * 16 + NKB],
                                                                 in1=ssq_pe[:, 0:NKB], op=OP.add),
                          reads=[PS[2], "ssq_pe"], writes=["rks"])
                    cx.op("act", lambda hl=hl: act.activation(out=rks[:, hl * 16:hl * 16 + NKB], in_=rks[:, hl * 16:hl * 16 + NKB],
                                                              func=AF.Ln, scale=1.0 / 192, bias=EPS), reads=["rks"], writes=["rks"])
                    cx.op("act", lambda hl=hl: act.activation(out=rks[:, hl * 16:hl * 16 + NKB], in_=rks[:, hl * 16:hl * 16 + NKB],
                                                              func=AF.Exp, scale=-0.5, bias=float(np.log(SCALE))),
                          reads=["rks"], writes=["rks"])
                for kb in range(NKB):
                    b = (3, 4, 0, 1)[kb % 4]
                    cx.op("pe", [lambda k=k, kb=kb, b=b: pe.matmul(
                        ps[b][:, :], lhsT=kvn[:, k, kb * 128:(kb + 1) * 128], rhs=wkvt[:, 2048 + k * 512:2048 + (k + 1) * 512],
                        start=(k == 0), stop=(k == 3)) for k in range(4)], reads=[wkvk, "kvn"], writes=[PS[b]])
                    cx.op("dve", lambda kb=kb, b=b: dve.tensor_copy(out=vtok[:, kb, :], in_=ps[b][:, :]), reads=[PS[b]], writes=["vtok"])
                bg_work(strm, 3, 5)
                def q_prod(hl, wqt, wqk):
                    qb = hl % 2
                    base = (hl % 2) * 2048
                    cx.op("pe", [lambda k=k: pe.matmul(
                        ps[0][:, :], lhsT=wqt[:, base + k * 256:base + k * 256 + 128], rhs=qlatn[:, k, :],
                        start=(k == 0), stop=(k == 7)) for k in range(8)], reads=[wqk, "qlatn"], writes=[PS[0]])
                    cx.op("pe", [lambda k=k: pe.matmul(
                        ps[1][0:64, :], lhsT=wqt[:, base + k * 256 + 128:base + k * 256 + 192], rhs=qlatn[:, k, :],
                        start=(k == 0), stop=(k == 7)) for k in range(8)], reads=[wqk, "qlatn"], writes=[PS[1]])
                    cx.op("pe", [lambda k=k: pe.matmul(
                        ps[3][0:64, :], lhsT=wqt[:, base + k * 256 + 192:base + k * 256 + 256], rhs=qlatn[:, k, :],
                        start=(k == 0), stop=(k == 7)) for k in range(8)], reads=[wqk, "qlatn"], writes=[PS[3]])
                    cx.op("act", lambda: act.activation(out=sqk[0], in_=ps[0][:, :], func=AF.Square), reads=[PS[0]], writes=["sqk0"])
                    cx.op("act", lambda: act.activation(out=sqr, in_=ps[1][0:64, :], func=AF.Square), reads=[PS[1]], writes=["sqr"])
                    cx.op("pe", [lambda: pe.matmul(ps[4][:, :], lhsT=ones_bf[:, :], rhs=sqk[0], start=True, stop=False),
                                 lambda: pe.matmul(ps[4][:, :], lhsT=ones_bf[0:64, :], rhs=sqr, start=False, stop=True)],
                          reads=["sqk0", "sqr", "ones_bf"], writes=[PS[4]])
                    rsqrt_act(rq, "rq", ps[4][:, :], PS[4], 1.0 / 192)
                    cx.op("dve", lambda: dve.scalar_tensor_tensor(out=qTn[qb], in0=ps[0][:, :], scalar=smalls[:, gq0:gq0 + 1],
                                                                  in1=rq, op0=OP.mult, op1=OP.mult),
                          reads=[PS[0], "rq", "smalls"], writes=[f"qTn{qb}"])
                    cx.op("dve", lambda: dve.tensor_tensor(out=r1, in0=ps[1][0:64, :], in1=Cq[:, tok0:tok0 + 512], op=OP.mult),
                          reads=[PS[1], "Cq"], writes=["r1"])
                    cx.op("dve", lambda: dve.tensor_tensor(out=r2, in0=ps[3][0:64, :], in1=Sq[:, tok0:tok0 + 512], op=OP.mult),
                          reads=[PS[3], "Sq"], writes=["r2"])
                    cx.op("dve", lambda: dve.tensor_tensor(out=r1, in0=r1, in1=r2, op=OP.add), reads=["r1", "r2"], writes=["r1"])
                    cx.op("dve", lambda: dve.tensor_tensor(out=qTr[qb], in0=r1, in1=rq[0:64, :], op=OP.mult),
                          reads=["r1", "rq"], writes=[f"qTr{qb}"])

                def jlo_of(kb):
                    return max(4 * hf, kb // 2) - 4 * hf

                def attn_pair(hls):
                    cfgs = []
                    for idx, hl_ in enumerate(hls):
                        cfgs.append(dict(hl=hl_, qb=hl_ % 2, h=hg * G + hl_, sb=(0, 1) if idx == 0 else (5, 6),
                                         ob=4 if idx == 0 else 7, smb=2 if idx == 0 else 3,
                                         pT=pT if idx == 0 else pTB, pk="pT" if idx == 0 else "pTB"))

                    def emit_scores(c, kb):
                        hl_, qb = c["hl"], c["qb"]
                        c0 = jlo_of(kb) * 128
                        b = c["sb"][kb % 2]
                        pi = kb % 3
                        pt = c["pT"][pi]
                        pk = f"{c['pk']}{pi}"
                        cx.op("pe", [lambda: pe.matmul(ps[b][:, c0:512], lhsT=kT[:, hl_, kb * 128:(kb + 1) * 128],
                                                       rhs=qTn[qb][:, c0:512], start=True, stop=False),
                                     lambda: pe.matmul(ps[b][:, c0:512], lhsT=kpe_rot[:, kb * 128:(kb + 1) * 128],
                                                       rhs=qTr[qb][:, c0:512], start=False, stop=True)],
                              reads=["kT", f"qTn{qb}", f"qTr{qb}", "kpe_rot"], writes=[PS[b]])
                        cx.op("act", lambda: act.activation(out=pt[:, c0:512], in_=ps[b][:, c0:512], func=AF.Exp,
                                                            scale=rks[:, hl_ * 16 + kb:hl_ * 16 + kb + 1], bias=negC[:, 0:1]),
                              reads=[PS[b], "rks", "negC"], writes=[pk])
                        Jm = kb // 2
                        if 4 * hf <= Jm < 4 * hf + 4:
                            jm = Jm - 4 * hf
                            mi = Jm * 2 + kb % 2
                            cx.op("dve", lambda: dve.tensor_tensor(out=pt[:, jm * 128:(jm + 1) * 128],
                                                                   in0=pt[:, jm * 128:(jm + 1) * 128],
                                                                   in1=maskb[:, mi * 128:(mi + 1) * 128], op=OP.mult),
                                  reads=[pk, "maskb"], writes=[pk])

                    def emit_pv(c, kb):
                        hl_ = c["hl"]
                        c0 = jlo_of(kb) * 128
                        pi = kb % 3
                        pt = c["pT"][pi]
                        pk = f"{c['pk']}{pi}"
                        ob, smb = c["ob"], c["smb"]
                        cx.op("pe", [lambda: pe.matmul(ps[ob][:, c0:512], lhsT=vtok[:, kb, hl_ * 128:(hl_ + 1) * 128],
                                                       rhs=pt[:, c0:512], start=(kb == 0), stop=(kb == NKB - 1),
                                                       skip_group_check=True),
                                     lambda: pe.matmul(ps[smb][:, c0:512], lhsT=ones_bf[:, :], rhs=pt[:, c0:512],
                                                       start=(kb == 0), stop=(kb == NKB - 1), skip_group_check=True)],
                              reads=["vtok", pk, "ones_bf"], writes=[PS[ob], PS[smb]])

                    for kb in range(NKB):
                        for c in cfgs:
                            emit_scores(c, kb)
                        if kb > 0:
                            for c in cfgs:
                                emit_pv(c, kb - 1)
                    for c in cfgs:
                        emit_pv(c, NKB - 1)
                    for c in cfgs:
                        ob, smb, h = c["ob"], c["smb"], c["h"]
                        cx.op("act", lambda smb=smb: act.activation(out=rin, in_=ps[smb][:, :], func=AF.Ln), reads=[PS[smb]], writes=["rin"])
                        cx.op("act", lambda: act.activation(out=rin, in_=rin, func=AF.Exp, scale=-1.0), reads=["rin"], writes=["rin"])
                        cx.op("dve", lambda ob=ob, h=h: dve.tensor_tensor(out=oT[:, h, :], in0=ps[ob][:, :], in1=rin, op=OP.mult),
                              reads=[PS[ob], "rin"], writes=["oT"])

                wq0, wq0k = strm.get()
                q_prod(0, wq0, wq0k)
                q_prod(1, wq0, wq0k)
                bg_work(strm, 3, 5)
                attn_pair((0, 1))
                wq1, wq1k = strm.get()
                q_prod(2, wq1, wq1k)
                q_prod(3, wq1, wq1k)
                bg_work(strm, 3, 5)
                attn_pair((2, 3))
            if d0:
                dbg_out("d_oT", oT.rearrange("p a b -> p (a b)"), "oT")
            cx.barrier(["dbg"])
            if stage < 4:
                continue

            vt = view("C", 0, BF16, [4, GMW])
            junk = view("C", 16384, BF16, [GMW])
            vss = view("C", 20480, F32, [8])
            mt = view("C", 20992, F32, [128])
            for ft in range(16):
                wt, wk = strm.get()
                b = ft % 2
                cx.op("pe", [lambda k=k, wt=wt, b=b: pe.matmul(ps[b][:, :], lhsT=wt[:, k * 128:(k + 1) * 128], rhs=hT[:, k, :],
                                                               start=(k == 0), stop=(k == 31)) for k in range(32)],
                      reads=[wk, "hT"], writes=[PS[b]])
                cx.op("act", lambda ft=ft, b=b: act.activation(out=gmT[:, ft, :], in_=ps[b][:, :], func=AF.Gelu),
                      reads=[PS[b]], writes=["gmT"])
            for cg in range(4):
                for s4 in range(4):
                    wt, wk = strm.get()
                    for tb in range(4):
                        cx.op("pe", [lambda kk=kk, wt=wt, tb=tb, s4=s4: pe.matmul(
                            ps[2 + tb][:, :], lhsT=hT[:, s4 * 8 + kk, tb * 128:(tb + 1) * 128], rhs=wt[:, kk * 512:(kk + 1) * 512],
                            start=(s4 == 0 and kk == 0), stop=(s4 == 3 and kk == 7)) for kk in range(8)],
                            reads=[wk, "hT"], writes=[PS[2 + tb]])
                for tb in range(4):
                    cx.op("act", lambda tb=tb, cg=cg: act.activation(out=vt[:, tb, cg * 512:(cg + 1) * 512], in_=ps[2 + tb][:, :],
                                                                     func=AF.Gelu), reads=[PS[2 + tb]], writes=["vt"])
            for tb in range(4):
                cx.op("act", lambda tb=tb: act.activation(out=junk, in_=vt[:, tb, :], func=AF.Square, accum_out=vss[:, tb:tb + 1]),
                      reads=["vt"], writes=["junk", "vss"])
            cx.op("act", lambda: act.activation(out=vss[:, 4:8], in_=vss[:, 0:4], func=AF.Ln, scale=1.0 / GMW, bias=EPS),
                  reads=["vss"], writes=["vss"])
            cx.op("act", lambda: act.activation(out=vss[:, 4:8], in_=vss[:, 4:8], func=AF.Exp, scale=-0.5), reads=["vss"], writes=["vss"])
            for tb in range(4):
                cx.op("dve", lambda tb=tb: dve.tensor_scalar(out=vt[:, tb, :], in0=vt[:, tb, :], scalar1=vss[:, 4 + tb:5 + tb],
                                                             scalar2=None, op0=OP.mult), reads=["vt", "vss"], writes=["vt"])
            for tb in range(4):
                for c in range(16):
                    g = c // 2
                    b = c % 2
                    cx.op("pe", lambda tb=tb, c=c, g=g, b=b: pe.matmul(ps[b][:, 0:128], lhsT=vt[:, tb, c * 128:(c + 1) * 128],
                                                                       rhs=wsm[:, g * 128:(g + 1) * 128], start=True, stop=True),
                          reads=["vt", "wsm"], writes=[PS[b]])
                    cx.op("dve", lambda c=c, g=g, b=b: dve.scalar_tensor_tensor(
                        out=mt, in0=ps[b][:, 0:128], scalar=smc("gv", c), in1=bsb[:, g * 128:(g + 1) * 128],
                        op0=OP.mult, op1=OP.add), reads=[PS[b], "smalls", "bsb"], writes=["mt"])
                    cx.op("dve", lambda tb=tb, c=c: dve.tensor_tensor(out=gmT[:, c, tb * 128:(tb + 1) * 128], in0=mt,
                                                                      in1=gmT[:, c, tb * 128:(tb + 1) * 128], op=OP.mult),
                          reads=["mt", "gmT"], writes=["gmT"])
            if d0:
                dbg_out("d_gm", gmT.rearrange("p a b -> p (a b)"), "gmT")
            cx.barrier(["dbg"])
            if stage < 5:
                continue

            sga = view("A", 49152, F32, [512])
            sgb = view("A", 51200, F32, [512])
            ya = view("A", 53248, F32, [512])
            yb = view("A", 55296, F32, [512])
            for cp in range(16):
                wa, wak = strm.get()
                for ci in range(2):
                    p0_ = 4 * ci
                    cx.op("pe", [lambda k=k, ci=ci, p0_=p0_, wa=wa: pe.matmul(
                        ps[p0_][:, :], lhsT=wa[:, ci * 2048 + k * 128:ci * 2048 + (k + 1) * 128], rhs=gmT[:, k, :],
                        start=(k == 0), stop=(k == 15)) for k in range(16)], reads=[wak, "gmT"], writes=[PS[p0_]])
                for ci in range(2):
                    c = cp * 2 + ci
                    p0_ = 4 * ci
                    for (src, skey, off) in [(oT, "oT", 1), (hT, "hT", 2), (hT, "hT", 3)]:
                        wt_, wk_ = strm.get()
                        cx.op("pe", [lambda k=k, wt_=wt_, src=src, off=off, p0_=p0_: pe.matmul(
                            ps[p0_ + off][:, :], lhsT=wt_[:, k * 128:(k + 1) * 128], rhs=src[:, k, :],
                            start=(k == 0), stop=(k == 31)) for k in range(32)], reads=[wk_, skey], writes=[PS[p0_ + off]])
                    cx.op("act", lambda p0_=p0_: act.activation(out=sga, in_=ps[p0_ + 2][:, :], func=AF.Sigmoid),
                          reads=[PS[p0_ + 2]], writes=["sga"])
                    cx.op("act", lambda p0_=p0_: act.activation(out=sgb, in_=ps[p0_ + 3][:, :], func=AF.Sigmoid),
                          reads=[PS[p0_ + 3]], writes=["sgb"])
                    cx.op("dve", lambda p0_=p0_: dve.tensor_tensor(out=ya, in0=ps[p0_][:, :], in1=sga, op=OP.mult),
                          reads=[PS[p0_], "sga"], writes=["ya"])
                    cx.op("dve", lambda p0_=p0_: dve.tensor_tensor(out=yb, in0=ps[p0_ + 1][:, :], in1=sgb, op=OP.mult),
                          reads=[PS[p0_ + 1], "sgb"], writes=["yb"])
                    cx.op("dve", lambda c=c: dve.tensor_tensor(out=mixed[:, c, :], in0=ya, in1=yb, op=OP.add),
                          reads=["ya", "yb"], writes=["mixed"])
            if d0:
                dbg_out("d_mixed", mixed.rearrange("p a b -> p (a b)"), "mixed")
            cx.barrier(["dbg"])
            if stage < 6:
                continue

            xs = [view("B", i * 8192, F32, [4, 512]) for i in range(2)]
            sq2 = [view("B", 16384 + i * 1024, BF16, [512]) for i in range(2)]
            for n in range(32):
                xi = (n // 4) % 2
                if n % 4 == 0:
                    cx.dma("sp", lambda n=n, xi=xi: sp.dma_start(out=xs[xi], in_=xo[:, n:n + 4, tok0:tok0 + 512]),
                           f"xs{xi}", writes=[f"xs{xi}"])
                wt, wk = strm.get()
                b = n % 2
                cx.op("pe", [lambda k=k, wt=wt, b=b: pe.matmul(ps[b][:, :], lhsT=wt[:, k * 128:(k + 1) * 128], rhs=mixed[:, k, :],
                                                               start=(k == 0), stop=(k == 31)) for k in range(32)],
                      reads=[wk, "mixed"], writes=[PS[b]])
                cx.op("dve", lambda n=n, b=b, xi=xi: dve.scalar_tensor_tensor(
                    out=acc[:, n, :], in0=ps[b][:, :], scalar=mod[:, 64 + n:65 + n], in1=xs[xi][:, n % 4, :],
                    op0=OP.mult, op1=OP.add), reads=[PS[b], "mod_b", f"xs{xi}"], writes=[RAK[n // 8]])
                cx.op("act", lambda n=n: act.activation(out=sq2[n % 2], in_=acc[:, n, :], func=AF.Square),
                      reads=[RAK[n // 8]], writes=[f"sq2_{n % 2}"])
                cx.op("pe", lambda n=n: pe.matmul(ps[2][:, :], lhsT=ones_bf[:, :], rhs=sq2[n % 2], start=(n == 0), stop=(n == 31)),
                      reads=[f"sq2_{n % 2}", "ones_bf"], writes=[PS[2]])
            if d0:
                dbg_out("d_x1", acc.rearrange("p a b -> p (a b)"), RAK)
            cx.barrier(["dbg"])
            rsqrt_act(rstdv, "rstd", ps[2][:, :], PS[2], 1.0 / D)
            norm_apply(acc, RAK, h2T, "h2T", a2, "a2", 96, rstdv, "rstd", tmpv, ["tmp0", "tmp1"])
            if d0:
                dbg_out("d_h2", h2T.rearrange("p a b -> p (a b)"), "h2T")
            cx.barrier(["dbg"])
            if stage < 7:
                continue

            aT = [view("C", i * 8192, BF16, [8, 512]) for i in range(2)]
            rl = [view("C", 16384 + i * 2048, F32, [512]) for i in range(2)]
            for fb in range(16):
                ab = aT[fb % 2]
                abk = f"aT{fb % 2}"
                for ft in range(8):
                    wt, wk = strm.get()
                    b = ft % 2
                    cx.op("pe", [lambda k=k, wt=wt, b=b: pe.matmul(ps[b][:, :], lhsT=wt[:, k * 128:(k + 1) * 128], rhs=h2T[:, k, :],
                                                                   start=(k == 0), stop=(k == 31)) for k in range(32)],
                          reads=[wk, "h2T"], writes=[PS[b]])
                    cx.op("act", lambda b=b: act.activation(out=rl[b], in_=ps[b][:, :], func=AF.Relu), reads=[PS[b]], writes=[f"rl{b}"])
                    cx.op("dve", lambda ft=ft, b=b, ab=ab: dve.tensor_tensor(out=ab[:, ft, :], in0=rl[b], in1=rl[b], op=OP.mult),
                          reads=[f"rl{b}"], writes=[abk])
                for s8 in range(8):
                    wt, wk = strm.get()
                    for nn in range(4):
                        n = s8 * 4 + nn
                        b = 2 + n % 4
                        cx.op("pe", [lambda k=k, wt=wt, nn=nn, b=b, ab=ab: pe.matmul(
                            ps[b][:, :], lhsT=wt[:, nn * 1024 + k * 128:nn * 1024 + (k + 1) * 128], rhs=ab[:, k, :],
                            start=(k == 0), stop=(k == 7)) for k in range(8)], reads=[wk, abk], writes=[PS[b]])
                        cx.op("dve", lambda n=n, b=b: dve.scalar_tensor_tensor(
                            out=acc[:, n, :], in0=ps[b][:, :], scalar=mod[:, 160 + n:161 + n], in1=acc[:, n, :],
                            op0=OP.mult, op1=OP.add), reads=[PS[b], "mod_b", RAK[n // 8]], writes=[RAK[n // 8]])
            for kq in range(4):
                cx.dma("sp", lambda kq=kq: sp.dma_start(out=ov[:, kq * 8:(kq + 1) * 8, tok0:tok0 + 512], in_=acc[:, kq * 8:(kq + 1) * 8, :]),
                       f"out{kq}", reads=[RAK[kq]])
            if hf + 1 < nhalf:
                tn = (hf + 1) * 512
                for kq in range(4):
                    cx.dma("sp", lambda kq=kq, tn=tn: sp.dma_start(out=xT[:, kq * 8:(kq + 1) * 8, :], in_=xo[:, kq * 8:(kq + 1) * 8, tn:tn + 512]),
                           f"xT{kq}", writes=[RAK[kq]])
            cx.barrier(["dbg"])
        fin = [s for s in ("out0", "out1", "out2", "out3", "dbg") if s in cx.dsem]
        cx.final_wait("sp", fin)
        print(f"[build] waits={cx.nwait} pe={E['pe']['cnt']} act={E['act']['cnt']} dve={E['dve']['cnt']} "
              f"ring_dmas={rstate['issued']}")
    return nc


def _t1(W, col0, ncols):
    K = W.shape[0]
    blk = W[:, col0:col0 + ncols].reshape(K // 128, 128, ncols // 128, 128)
    return np.ascontiguousarray(blk.transpose(2, 1, 0, 3)).reshape(ncols // 128, 128, (K // 128) * 128)


_WCACHE = {}


def _prep_weights(w_ada, w_in, w_uq, w_ukv, w_branch_a, w_branch_b, w_out, w_ff1, w_ff2):
    WA = np.ascontiguousarray(w_ada.reshape(4, 8, 128, 48, 512).transpose(3, 0, 2, 1, 4)).reshape(192, 128, 4096)
    sw = (np.arange(64) + 32) % 64
    WKV = np.empty((5, 128, 4096), np.float32)
    WKV[0:4] = _t1(w_in, OFF_KV, 512)
    kp = w_in[:, OFF_KPE:OFF_KPE + 64]
    WKV[4] = _t1(np.concatenate([kp, kp[:, sw]], axis=1), 0, 128)[0]
    WS = np.empty((NSLOT_HALF, 128, 4096), np.float32)
    i = 0
    WS[i:i + 8] = _t1(w_in, OFF_Q, QL); i += 8
    wk4 = w_ukv.reshape(4, 128, HEADS, 256)
    wq8 = w_uq.reshape(8, 128, HEADS, 192)
    for hg in range(HEADS // G):
        hs = slice(hg * G, (hg + 1) * G)
        kpart = wk4[:, :, hs, 0:128].transpose(1, 2, 0, 3).reshape(128, G * 4 * 128)
        vpart = wk4[:, :, hs, 128:256].transpose(1, 0, 2, 3).reshape(128, 4 * G * 128)
        WS[i] = np.concatenate([kpart, vpart], axis=1); i += 1
        for hp in range(G // 2):
            parts = []
            for h2i in range(2):
                h = hg * G + hp * 2 + h2i
                q = wq8[:, :, h, :]
                blk = np.concatenate([q[:, :, 0:128], q[:, :, 128:192], q[:, :, 128:192][:, :, sw]], axis=2)
                parts.append(blk.transpose(1, 0, 2).reshape(128, 8 * 256))
            WS[i] = np.concatenate(parts, axis=1); i += 1
    WS[i:i + 16] = _t1(w_in, 0, GMW); i += 16
    for cg in range(4):
        blk = w_in[:, GMW + cg * 512:GMW + (cg + 1) * 512].reshape(4, 8, 128, 512)
        WS[i:i + 4] = blk.transpose(0, 2, 1, 3).reshape(4, 128, 4096); i += 4
    TA = _t1(w_branch_a, 0, D)
    TB = _t1(w_branch_b, 0, D)
    TGA = _t1(w_in, OFF_GATE, D)
    TGB = _t1(w_in, OFF_GATE + D, D)
    for cp in range(16):
        WS[i] = np.concatenate([TA[2 * cp], TA[2 * cp + 1]], axis=1); i += 1
        for ci in range(2):
            c = 2 * cp + ci
            WS[i] = TB[c]; WS[i + 1] = TGA[c]; WS[i + 2] = TGB[c]; i += 3
    WS[i:i + 32] = _t1(w_out, 0, D); i += 32
    T1f = _t1(w_ff1, 0, DFF)
    for fb in range(16):
        WS[i:i + 8] = T1f[fb * 8:(fb + 1) * 8]; i += 8
        blk = w_ff2[fb * 1024:(fb + 1) * 1024, :].reshape(8, 128, 8, 4, 128)
        WS[i:i + 8] = blk.transpose(2, 1, 3, 0, 4).reshape(8, 128, 4096); i += 8
    assert i == NSLOT_HALF, i
    return WA, WKV, WS


def kernel(x, c, positions, w_ada, b_ada, g_norm1, w_in, g_v, w_s, b_s, g_q_lat, g_kv_lat,
           w_uq, w_ukv, g_qn, g_kn, w_branch_a, w_branch_b, w_out, g_norm2, w_ff1, w_ff2):
    f = lambda a: np.asarray(a, dtype=np.float32)
    x = f(x); c = f(c); positions = np.asarray(positions, dtype=np.int32)
    WA, WKV, WS = _prep_weights(f(w_ada)[0], f(w_in)[0], f(w_uq)[0], f(w_ukv)[0], f(w_branch_a)[0],
                                f(w_branch_b)[0], f(w_out)[0], f(w_ff1)[0], f(w_ff2)[0])
    sw = (np.arange(64) + 32) % 64
    gq = f(g_qn)[0]; gk = f(g_kn)[0]
    invf = (1.0 / (np.float32(10000.0) ** (np.arange(0, 64, 2, dtype=np.float32) / np.float32(64)))).astype(np.float32)

    def gcol(g):
        o = np.zeros((128, 3), np.float32)
        o[:, 0] = g[0:128]; o[0:64, 1] = g[128:192]; o[0:64, 2] = g[128:192][sw]
        return o

    wsT = np.ascontiguousarray(f(w_s)[0].transpose(2, 0, 1)).reshape(128, 1024)
    tri = (np.arange(128)[:, None] <= np.arange(128)[None, :]).astype(np.float32)
    bsb = np.ascontiguousarray(np.broadcast_to(f(b_s)[0].reshape(1, 1024), (128, 1024)))
    grow = np.concatenate([gq, gk]).reshape(1, 384).astype(np.float32)
    in_maps = []
    for core in range(8):
        b, par = core // 2, core % 2
        own = np.concatenate([np.arange(blk * 128, (blk + 1) * 128) for blk in BL[par]])
        sm = np.zeros((128, NSM), np.float32)

        def put(name, arr):
            o, w = SM[name]
            sm[0:arr.shape[0], o:o + arr.shape[1]] = arr

        put("cT", c[b].reshape(32, 128).T)
        put("b_ada", f(b_ada)[0].reshape(192, 128).T)
        put("g1", f(g_norm1)[0].reshape(32, 128).T)
        put("g2", f(g_norm2)[0].reshape(32, 128).T)
        put("gql", f(g_q_lat)[0].reshape(8, 128).T)
        put("gkvl", f(g_kv_lat)[0].reshape(4, 128).T)
        put("gv", f(g_v)[0].reshape(16, 128).T)
        put("gqn", gcol(gq)); put("gkn", gcol(gk))
        put("invf", np.concatenate([invf, invf]).reshape(64, 1))
        put("sgn", np.concatenate([-np.ones(32), np.ones(32)]).astype(np.float32).reshape(64, 1))
        maskT = np.zeros((128, 16, 128), np.float32)
        for J in range(8):
            for m in range(2):
                kb = 2 * J + m
                Bq = BL[par][J]
                if kb < Bq:
                    maskT[:, J * 2 + m, :] = 1.0
                elif kb == Bq:
                    maskT[:, J * 2 + m, :] = tri
        in_maps.append({
            "xTb": np.ascontiguousarray(x[b].T), "xTo": np.ascontiguousarray(x[b][own].T),
            "posb": np.ascontiguousarray(np.broadcast_to(positions[b][None, :], (64, S))),
            "poso": np.ascontiguousarray(np.broadcast_to(positions[b][own][None, :], (64, 1024))),
            "smalls": sm, "grow": grow, "wsT": wsT, "tri": tri, "bsb": bsb,
            "maskT": maskT.reshape(128, 2048), "WA": WA, "WKV": WKV, "WS": WS,
        })
    nc = build_program()
    res = run_bass_kernel_spmd(nc, in_maps, core_ids=list(range(8)))
    out = np.empty((NB, S, D), np.float32)
    for core in range(8):
        b, par = core // 2, core % 2
        own = np.concatenate([np.arange(blk * 128, (blk + 1) * 128) for blk in BL[par]])
        out[b, own, :] = np.asarray(res.results[core]["outT"]).T
    kernel.last_results = res.results
    return out
```

```python
import os
import contextlib
import numpy as np
import concourse.bass as bass
import concourse.mybir as mybir
from concourse.bass_utils import run_bass_kernel_spmd

F32 = mybir.dt.float32
BF16 = mybir.dt.bfloat16
I32 = mybir.dt.int32
AF = mybir.ActivationFunctionType
OP = mybir.AluOpType
AX = mybir.AxisListType

D = 4096
S = 2048
NB = 4
GMW = 2048
HEADS = 32
QL = 1024
KVL = 512
DFF = 16384
OFF_Q = 4096
OFF_KV = OFF_Q + QL
OFF_KPE = OFF_KV + KVL
OFF_GATE = OFF_KPE + 64
EPS = 1e-6
SCALE = 192.0 ** -0.5
BL = [[0, 3, 4, 7, 8, 11, 12, 15], [1, 2, 5, 6, 9, 10, 13, 14]]
NRING = 4
G = 4
TWO_PI = 2.0 * np.pi
C1 = 6.28125
C2 = float(TWO_PI - C1)
MAGIC = 12582912.0

SM = {}
_o = 0
for _n, _w in [("cT", 32), ("b_ada", 192), ("g1", 32), ("g2", 32), ("gql", 8), ("gkvl", 4), ("gv", 16),
               ("gqn", 3), ("gkn", 3), ("invf", 1), ("sgn", 1)]:
    SM[_n] = (_o, _w)
    _o += _w
NSM = _o

DEBUG = os.environ.get("MK_DEBUG", "")


class Ctx:
    def __init__(self, nc, es):
        self.nc = nc
        self.es = es
        self.eng = {}
        for name, obj in [("pe", nc.tensor), ("act", nc.scalar), ("dve", nc.vector),
                          ("pool", nc.gpsimd), ("sp", nc.sync)]:
            sem = es.enter_context(nc.semaphore("sem_" + name))
            self.eng[name] = {"o": obj, "sem": sem, "cnt": 0, "known": {}, "name": name}
        self.w = {}
        self.r = {}
        self.dsem = {}
        self.dcnt = {}
        self.nwait = 0

    def _need(self, e, ev):
        if ev is None:
            return
        sem, val = ev
        k = id(sem)
        if e["known"].get(k, 0) >= val:
            return
        e["o"].wait_ge(sem, val)
        self.nwait += 1
        e["known"][k] = val

    def _deps(self, e, reads, writes):
        for k in reads:
            self._need(e, self.w.get(k))
            if k.startswith("ps"):
                for ev in self.r.get(k, ()):
                    if ev[0] is not e["sem"]:
                        self._need(e, ev)
        for k in writes:
            self._need(e, self.w.get(k))
            for ev in self.r.get(k, ()):
                if ev[0] is e["sem"]:
                    continue
                self._need(e, ev)

    def _commit(self, ev, reads, writes):
        for k in writes:
            self.w[k] = ev
            self.r[k] = []
        for k in reads:
            self.r.setdefault(k, []).append(ev)

    def op(self, en, emits, reads=(), writes=()):
        e = self.eng[en]
        self._deps(e, reads, writes)
        if callable(emits):
            emits = [emits]
        ins = None
        for f in emits:
            ins = f()
        e["cnt"] += 1
        ins.then_inc(e["sem"], 1)
        ev = (e["sem"], e["cnt"])
        self._commit(ev, reads, writes)

    def dma(self, q, emit, semname, reads=(), writes=()):
        e = self.eng[q]
        self._deps(e, reads, writes)
        if semname not in self.dsem:
            self.dsem[semname] = self.es.enter_context(self.nc.semaphore("d_" + semname))
            self.dcnt[semname] = 0
        ins = emit()
        self.dcnt[semname] += 16
        ins.then_inc(self.dsem[semname], 16)
        ev = (self.dsem[semname], self.dcnt[semname])
        self._commit(ev, reads, writes)

    def barrier(self, extra_dsems=()):
        evs = [(self.eng[n]["sem"], self.eng[n]["cnt"]) for n in ("pe", "act", "dve") if self.eng[n]["cnt"] > 0]
        for s in extra_dsems:
            if s in self.dsem:
                evs.append((self.dsem[s], self.dcnt[s]))
        for n in ("pe", "act", "dve", "sp"):
            for ev in evs:
                if ev[0] is self.eng[n]["sem"]:
                    continue
                self._need(self.eng[n], ev)

    def final_wait(self, q, semnames):
        for s in semnames:
            self._need(self.eng[q], (self.dsem[s], self.dcnt[s]))


NSLOT_HALF = 8 + 24 + 16 + 16 + 112 + 32 + 256


def build_program():
    nc = bass.Bass("TRN2", target_bir_lowering=False)
    dr = {}

    def din(name, shape, dt=F32):
        dr[name] = nc.dram_tensor(name, shape, dt, kind="ExternalInput").ap()

    din("xTb", [D, S]); din("xTo", [D, 1024]); din("posb", [64, S], I32); din("poso", [64, 1024], I32)
    din("smalls", [128, NSM]); din("grow", [1, 384]); din("wsT", [128, 1024]); din("tri", [128, 128])
    din("bsb", [128, 1024]); din("maskT", [128, 2048]); din("WA", [192, 128, 4096]); din("WKV", [5, 128, 4096])
    din("WS", [NSLOT_HALF, 128, 4096])
    dr["outT"] = nc.dram_tensor("outT", [D, 1024], F32, kind="ExternalOutput").ap()
    dbg = {}
    if DEBUG:
        for nm, shp, dt in [("d_mod", [128, 192], F32), ("d_kvn", [128, 4 * S], BF16), ("d_kpe", [64, S], BF16),
                            ("d_ssqpe", [128, 16], F32), ("d_negC", [128, 1], F32),
                            ("d_hT", [128, 32 * 512], BF16), ("d_qlatn", [128, 8 * 512], BF16),
                            ("d_oT", [128, 32 * 512], BF16), ("d_gm", [128, 16 * 512], BF16),
                            ("d_mixed", [128, 32 * 512], BF16), ("d_x1", [128, 32 * 512], F32),
                            ("d_h2", [128, 32 * 512], BF16)]:
            dbg[nm] = nc.dram_tensor(nm, shp, dt, kind="ExternalOutput").ap()

    with contextlib.ExitStack() as es:
        cx = Ctx(nc, es)
        E = cx.eng
        pe, act, dve, pool, sp = (E[n]["o"] for n in ("pe", "act", "dve", "pool", "sp"))

        def sb(name, shape, dt):
            return es.enter_context(nc.sbuf_tensor("s_" + name, shape, dt))

        ps = [es.enter_context(nc.psum_tensor(f"ps{i}", [128, 512], F32)) for i in range(8)]
        PS = [f"ps{i}" for i in range(8)]
        ring = [sb(f"ring{i}", [128, 4096], BF16) for i in range(NRING)]
        RA = sb("RA", [128, 16384], F32)
        RB = sb("RB", [128, 16384], BF16)
        RC = sb("RC", [128, 16384], BF16)
        smalls = sb("smalls", [128, NSM], F32)
        mod = sb("mod", [128, 192], F32)
        cond_bf = sb("cond_bf", [128, 32], BF16)
        mrow_acc = sb("mrow_acc", [1, 512], F32)
        a1 = sb("a1", [128, 32], F32)
        a2 = sb("a2", [128, 32], F32)
        ones_bf = sb("ones_bf", [128, 128], BF16)
        ones_f = sb("ones_f", [128, 128], F32)
        negC = sb("negC", [128, 1], F32)
        kvn = sb("kvn", [128, 4, S], BF16)
        kpe_rot = sb("kpe_rot", [64, S], BF16)
        ssq_pe = sb("ssq_pe", [128, 16], F32)
        Cq = sb("Cq", [64, 1024], F32)
        Sq = sb("Sq", [64, 1024], F32)
        maskb = sb("maskb", [128, 2048], BF16)
        wsm = sb("wsm", [128, 1024], BF16)
        bsb = sb("bsb", [128, 1024], F32)
        regs = {"A": (RA, 4), "B": (RB, 2), "C": (RC, 2)}

        def view(reg, off, dt, dims):
            base, esz = regs[reg]
            dsz = 4 if dt in (F32, I32) else 2
            n = 1
            for d_ in dims:
                n *= d_
            nb = n * dsz
            assert off % 4 == 0 and off + nb <= 16384 * esz, (reg, off, nb)
            v = base[:, off // esz:(off + nb) // esz]
            bdt = F32 if reg == "A" else BF16
            if dt != bdt:
                v = v.bitcast(dt)
            if len(dims) == 2:
                v = v.rearrange("p (a b) -> p a b", b=dims[1])
            return v

        def smc(name, j=0, w=1, rows=128):
            o, _ = SM[name]
            return smalls[0:rows, o + j:o + j + w]

        rstate = {"issued": 0}

        def ring_issue(src_ap):
            i = rstate["issued"] % NRING
            rstate["issued"] += 1
            t = ring[i]
            cx.dma("pool", lambda: pool.dma_start(out=t[:, :].rearrange("p (a b) -> p a b", b=2048),
                                                  in_=src_ap.rearrange("p (a b) -> p a b", b=2048)),
                   f"ring{i}", writes=[f"ring{i}"])
            return i

        class Stream:
            def __init__(self, srcs):
                self.srcs = srcs
                self.nxt = 0
                self.q = []

            def fill(self):
                while self.nxt < len(self.srcs) and len(self.q) < NRING - 1:
                    self.q.append(ring_issue(self.srcs[self.nxt]))
                    self.nxt += 1

            def get(self):
                self.fill()
                i = self.q.pop(0)
                self.fill()
                return ring[i], f"ring{i}"

        def dbg_out(name, src, key):
            if DEBUG:
                cx.dma("sp", lambda: sp.dma_start(out=dbg[name], in_=src), "dbg", reads=key if isinstance(key, list) else [key])

        def rsqrt_act(dst, dkey, src, skey, scale):
            cx.op("act", lambda: act.activation(out=dst, in_=src, func=AF.Ln, scale=scale, bias=EPS),
                  reads=[skey], writes=[dkey])
            cx.op("act", lambda: act.activation(out=dst, in_=dst, func=AF.Exp, scale=-0.5), reads=[dkey], writes=[dkey])

        cx.dma("sp", lambda: sp.dma_start(out=smalls[:, :], in_=dr["smalls"]), "c_sm", writes=["smalls"])
        cx.dma("sp", lambda: sp.dma_start(out=bsb[:, :], in_=dr["bsb"]), "c_bsb", writes=["bsb"])
        cx.op("dve", lambda: dve.memset(ones_bf[:, :], 1.0), writes=["ones_bf"])
        cx.op("dve", lambda: dve.memset(ones_f[:, :], 1.0), writes=["ones_f"])

        tmpA = view("B", 0, F32, [2048])
        tmpB = view("B", 8192, F32, [1024])
        tri = view("B", 12288, F32, [128])
        grow = view("B", 12800, F32, [384])
        gm2 = view("B", 14336, F32, [4])
        cx.dma("sp", lambda: sp.dma_start(out=tmpA, in_=dr["maskT"]), "c_a", writes=["tmpA"])
        cx.dma("sp", lambda: sp.dma_start(out=tmpB, in_=dr["wsT"]), "c_b", writes=["tmpB"])
        cx.dma("sp", lambda: sp.dma_start(out=tri, in_=dr["tri"]), "c_c", writes=["tri"])
        cx.dma("sp", lambda: sp.dma_start(out=grow[0:1, :], in_=dr["grow"]), "c_d", writes=["grow"])
        cx.op("dve", lambda: dve.tensor_copy(out=maskb[:, :], in_=tmpA), reads=["tmpA"], writes=["maskb"])
        for g in range(8):
            cx.op("dve", lambda g=g: dve.tensor_tensor(out=wsm[:, g * 128:(g + 1) * 128], in0=tmpB[:, g * 128:(g + 1) * 128],
                                                       in1=tri, op=OP.mult), reads=["tmpB", "tri"], writes=["wsm"])
        cx.op("dve", lambda: dve.tensor_reduce(out=gm2[0:1, 0:1], in_=grow[0:1, 0:192], axis=AX.X, op=OP.max,
                                               apply_absolute_value=True), reads=["grow"], writes=["gm2a"])
        cx.op("dve", lambda: dve.tensor_reduce(out=gm2[0:1, 1:2], in_=grow[0:1, 192:384], axis=AX.X, op=OP.max,
                                               apply_absolute_value=True), reads=["grow"], writes=["gm2b"])
        cx.op("dve", lambda: dve.tensor_tensor(out=gm2[0:1, 2:3], in0=gm2[0:1, 0:1], in1=gm2[0:1, 1:2], op=OP.mult),
              reads=["gm2a", "gm2b"], writes=["gm2c"])
        cx.op("dve", lambda: dve.tensor_scalar(out=gm2[0:1, 3:4], in0=gm2[0:1, 2:3], scalar1=-(192.0 ** 0.5),
                                               scalar2=None, op0=OP.mult), reads=["gm2c"], writes=["gm2d"])
        cx.op("pe", lambda: pe.matmul(ps[0][:, 0:1], lhsT=ones_f[0:1, :], rhs=gm2[0:1, 3:4], start=True, stop=True),
              reads=["ones_f", "gm2d"], writes=[PS[0]])
        cx.op("dve", lambda: dve.tensor_copy(out=negC[:, :], in_=ps[0][:, 0:1]), reads=[PS[0]], writes=["negC"])
        dbg_out("d_negC", negC[:, :], "negC")

        Ck = view("C", 16384, BF16, [S])[0:64, :]
        Sk = view("C", 20480, BF16, [S])[0:64, :]
        sq4 = [view("C", 24576, BF16, [4, 512]), view("C", 28672, BF16, [4, 512])]

        def rope_tables(pos_ap, T, gname, Ct, St, ckey, skey, pfx, toff):
            posi = view("A", toff, I32, [T])[0:64, :]
            ang = view("A", toff + 4 * T, F32, [T])[0:64, :]
            t1 = view("A", toff + 8 * T, F32, [T])[0:64, :]
            t2 = view("A", toff + 12 * T, F32, [T])[0:64, :]
            kk = pfx
            cx.dma("sp", lambda: sp.dma_start(out=posi, in_=pos_ap), "c_" + pfx, writes=[kk + "posi"])
            cx.op("dve", lambda: dve.tensor_copy(out=ang, in_=posi), reads=[kk + "posi"], writes=[kk + "ang"])
            cx.op("dve", lambda: dve.tensor_scalar(out=ang, in0=ang, scalar1=smc("invf", rows=64), scalar2=None, op0=OP.mult),
                  reads=[kk + "ang", "smalls"], writes=[kk + "ang"])

            def reduce_sin(dst, dstkey):
                cx.op("dve", lambda: dve.tensor_scalar(out=t1, in0=ang, scalar1=1.0 / TWO_PI, scalar2=MAGIC,
                                                       op0=OP.mult, op1=OP.add), reads=[kk + "ang"], writes=[kk + "t1"])
                cx.op("dve", lambda: dve.tensor_scalar(out=t1, in0=t1, scalar1=-MAGIC, scalar2=None, op0=OP.add),
                      reads=[kk + "t1"], writes=[kk + "t1"])
                cx.op("dve", lambda: dve.scalar_tensor_tensor(out=t2, in0=t1, scalar=-C1, in1=ang, op0=OP.mult, op1=OP.add),
                      reads=[kk + "t1", kk + "ang"], writes=[kk + "t2"])
                cx.op("dve", lambda: dve.scalar_tensor_tensor(out=t2, in0=t1, scalar=-C2, in1=t2, op0=OP.mult, op1=OP.add),
                      reads=[kk + "t1", kk + "t2"], writes=[kk + "t2"])
                cx.op("dve", lambda: dve.tensor_scalar(out=t2, in0=t2, scalar1=-3.14159, scalar2=3.14159, op0=OP.max, op1=OP.min),
                      reads=[kk + "t2"], writes=[kk + "t2"])
                cx.op("act", lambda: act.activation(out=dst, in_=t2, func=AF.Sin), reads=[kk + "t2"], writes=[dstkey])

            reduce_sin(St, skey)
            cx.op("dve", lambda: dve.tensor_scalar(out=ang, in0=ang, scalar1=float(np.pi / 2), scalar2=None, op0=OP.add),
                  reads=[kk + "ang"], writes=[kk + "ang"])
            reduce_sin(Ct, ckey)
            o, _ = SM[gname]
            cx.op("dve", lambda: dve.tensor_scalar(out=Ct, in0=Ct, scalar1=smalls[0:64, o + 1:o + 2], scalar2=None, op0=OP.mult),
                  reads=[ckey, "smalls"], writes=[ckey])
            cx.op("dve", lambda: dve.tensor_scalar(out=St, in0=St, scalar1=smalls[0:64, o + 2:o + 3], scalar2=smc("sgn", rows=64),
                                                   op0=OP.mult, op1=OP.mult), reads=[skey, "smalls"], writes=[skey])

        rope_rec = []
        _op, _dma = cx.op, cx.dma
        cx.op = lambda *a, **k: rope_rec.append((_op, a, k))
        cx.dma = lambda *a, **k: rope_rec.append((_dma, a, k))
        rope_tables(dr["posb"], S, "gkn", Ck, Sk, "Ck", "Sk", "rk_", 0)
        rope_tables(dr["poso"], 1024, "gqn", Cq[:, :], Sq[:, :], "Cq", "Sq", "rq_", 32768)
        cx.op, cx.dma = _op, _dma

        o_c, _ = SM["cT"]
        o_b, _ = SM["b_ada"]
        cx.op("act", lambda: act.activation(out=cond_bf[:, :], in_=smalls[:, o_c:o_c + 32], func=AF.Silu),
              reads=["smalls"], writes=["cond"])
        WAd = dr["WA"]

        class BG:
            nxt = 64

        def bg_srcs(n):
            out = [WAd[BG.nxt + i] for i in range(n) if BG.nxt + i < 192]
            return out

        def ada_item(strm_, si, pbank):
            ct, kq = si // 4, si % 4
            wt, wk = strm_.get()
            cx.op("pe", [lambda kk=kk: pe.matmul(ps[pbank][0:1, :], lhsT=cond_bf[:, kq * 8 + kk:kq * 8 + kk + 1],
                                                 rhs=wt[:, kk * 512:(kk + 1) * 512], start=(kk == 0), stop=(kk == 7))
                         for kk in range(8)], reads=[wk, "cond"], writes=[PS[pbank]])
            if kq == 0:
                cx.op("act", lambda: act.copy(out=mrow_acc[0:1, :], in_=ps[pbank][0:1, :]), reads=[PS[pbank]], writes=["mrow"])
            else:
                cx.op("dve", lambda: dve.tensor_tensor(out=mrow_acc[0:1, :], in0=ps[pbank][0:1, :], in1=mrow_acc[0:1, :], op=OP.add),
                      reads=[PS[pbank], "mrow"], writes=["mrow"])
            if kq == 3:
                cx.op("pe", [lambda j=j: pe.matmul(ps[pbank][:, j:j + 1], lhsT=mrow_acc[0:1, j * 128:(j + 1) * 128],
                                                   rhs=ones_f[0:1, 0:1], start=True, stop=True) for j in range(4)],
                      reads=["mrow", "ones_f"], writes=[PS[pbank]])
                mk = "mod_a" if ct < 16 else "mod_b"
                cx.op("dve", lambda: dve.tensor_tensor(out=mod[:, ct * 4:ct * 4 + 4], in0=ps[pbank][:, 0:4],
                                                       in1=smalls[:, o_b + ct * 4:o_b + ct * 4 + 4], op=OP.add),
                      reads=[PS[pbank], "smalls"], writes=[mk])

        def bg_work(strm_, n, pbank):
            for _ in range(n):
                if BG.nxt >= 192:
                    return
                ada_item(strm_, BG.nxt, pbank)
                BG.nxt += 1
                if BG.nxt == 192:
                    og2, _ = SM["g2"]
                    cx.op("dve", lambda: dve.scalar_tensor_tensor(out=a2[:, :], in0=mod[:, 128:160], scalar=1.0,
                                                                  in1=smalls[:, og2:og2 + 32], op0=OP.add, op1=OP.mult),
                          reads=["mod_b", "smalls"], writes=["a2"])
                    dbg_out("d_mod", mod[:, :], "mod_b")

        astrm = Stream([WAd[si] for si in range(64)])
        astrm.fill()
        for si in range(64):
            ada_item(astrm, si, 1 + (si // 4) % 2)
            if rope_rec:
                f_, a_, k_ = rope_rec.pop(0)
                f_(*a_, **k_)
        while rope_rec:
            f_, a_, k_ = rope_rec.pop(0)
            f_(*a_, **k_)
        og1, _ = SM["g1"]
        cx.op("dve", lambda: dve.scalar_tensor_tensor(out=a1[:, :], in0=mod[:, 32:64], scalar=1.0, in1=smalls[:, og1:og1 + 32],
                                                      op0=OP.add, op1=OP.mult), reads=["mod_a", "smalls"], writes=["a1"])
        cx.barrier(["dbg"])

        RAK = ["ra0", "ra1", "ra2", "ra3"]

        def norm_stats(x3, xkey, sq, sqkeys, psb, nch, hook=None):
            ng = nch // 4
            for g in range(ng):
                cx.op("act", lambda g=g: act.activation(out=sq4[g % 2], in_=x3[:, 4 * g:4 * g + 4, :], func=AF.Square),
                      reads=[xkey[g // 2]], writes=[f"sq4_{g % 2}"])
                cx.op("pe", [lambda g=g, j=j: pe.matmul(ps[psb][:, :], lhsT=ones_bf[:, :], rhs=sq4[g % 2][:, j, :],
                                                        start=(g == 0 and j == 0), stop=(g == ng - 1 and j == 3)) for j in range(4)],
                      reads=[f"sq4_{g % 2}", "ones_bf"], writes=[PS[psb]])
                if hook:
                    hook()

        def norm_apply(x3, xkey, h3, hkey, avec, akey, shcol0, rstd, rkey, tmp, tkeys, hook=None):
            for k in range(32):
                cx.op("dve", lambda k=k: dve.scalar_tensor_tensor(out=tmp[k % 2], in0=x3[:, k, :], scalar=avec[:, k:k + 1],
                                                                  in1=rstd, op0=OP.mult, op1=OP.mult),
                      reads=[xkey[k // 8], akey, rkey], writes=[tkeys[k % 2]])
                cx.op("act", lambda k=k: act.activation(out=h3[:, k, :], in_=tmp[k % 2], func=AF.Identity,
                                                        bias=mod[:, shcol0 + k:shcol0 + k + 1]),
                      reads=[tkeys[k % 2], "mod_a" if shcol0 == 0 else "mod_b"], writes=[hkey])
                if hook and k % 4 == 3:
                    hook()

        xTa = view("A", 0, F32, [32, 512])
        hTb = view("B", 0, BF16, [32, 512])
        sqv = [view("C", 0, BF16, [512]), view("C", 1024, BF16, [512])]
        tmpv = [view("C", 2048, F32, [512]), view("C", 4096, F32, [512])]
        rstdv = view("C", 6144, F32, [512])
        kpr = view("C", 8192, F32, [512])[0:64, :]
        kpsw = view("C", 10240, F32, [512])[0:64, :]
        sqp = view("C", 12288, BF16, [512])[0:64, :]
        xvb = dr["xTb"].rearrange("(k p) t -> p k t", p=128)
        ksrcs = []
        _n = 64
        for _ in range(4):
            for _j in range(10):
                if _n < 192:
                    ksrcs.append(WAd[_n]); _n += 1
            for i in range(5):
                ksrcs.append(dr["WKV"][i])
        kstrm = Stream(ksrcs)
        kstrm.fill()
        for tt in range(4):
            for kq in range(4):
                cx.dma("sp", lambda tt=tt, kq=kq: sp.dma_start(out=xTa[:, kq * 8:(kq + 1) * 8, :],
                                                               in_=xvb[:, kq * 8:(kq + 1) * 8, tt * 512:(tt + 1) * 512]),
                       f"xT{kq}", writes=[RAK[kq]])
            kcalls = [0]

            def khook():
                kcalls[0] += 1
                if kcalls[0] in (1, 3, 5, 6, 8, 10, 11, 13, 15, 16):
                    bg_work(kstrm, 1, 6 + BG.nxt % 2)
            norm_stats(xTa, RAK, sqv, ["sq0", "sq1"], 0, 32, hook=khook)
            rsqrt_act(rstdv, "rstd", ps[0][:, :], PS[0], 1.0 / D)
            norm_apply(xTa, RAK, hTb, "hT", a1, "a1", 0, rstdv, "rstd", tmpv, ["tmp0", "tmp1"], hook=khook)
            for ft in range(4):
                wt, wk = kstrm.get()
                cx.op("pe", [lambda ft=ft, k=k, wt=wt: pe.matmul(ps[1 + ft][:, :], lhsT=wt[:, k * 128:(k + 1) * 128],
                                                                 rhs=hTb[:, k, :], start=(k == 0), stop=(k == 31))
                             for k in range(32)], reads=[wk, "hT"], writes=[PS[1 + ft]])
                cx.op("act", lambda ft=ft: act.activation(out=sqv[ft % 2], in_=ps[1 + ft][:, :], func=AF.Square),
                      reads=[PS[1 + ft]], writes=[f"sq{ft % 2}"])
                cx.op("pe", lambda ft=ft: pe.matmul(ps[5][:, :], lhsT=ones_bf[:, :], rhs=sqv[ft % 2], start=(ft == 0), stop=(ft == 3)),
                      reads=[f"sq{ft % 2}", "ones_bf"], writes=[PS[5]])
            rsqrt_act(rstdv, "rstd", ps[5][:, :], PS[5], 1.0 / KVL)
            for ft in range(4):
                cx.op("dve", lambda ft=ft, tt=tt: dve.scalar_tensor_tensor(
                    out=kvn[:, ft, tt * 512:(tt + 1) * 512], in0=ps[1 + ft][:, :], scalar=smc("gkvl", ft), in1=rstdv,
                    op0=OP.mult, op1=OP.mult), reads=[PS[1 + ft], "rstd", "smalls"], writes=["kvn"])
            wt, wk = kstrm.get()
            cx.op("pe", [lambda k=k, wt=wt: pe.matmul(ps[6][0:64, :], lhsT=wt[:, k * 128:k * 128 + 64], rhs=hTb[:, k, :],
                                                      start=(k == 0), stop=(k == 31)) for k in range(32)],
                  reads=[wk, "hT"], writes=[PS[6]])
            cx.op("pe", [lambda k=k, wt=wt: pe.matmul(ps[7][0:64, :], lhsT=wt[:, k * 128 + 64:k * 128 + 128], rhs=hTb[:, k, :],
                                                      start=(k == 0), stop=(k == 31)) for k in range(32)],
                  reads=[wk, "hT"], writes=[PS[7]])
            cx.op("act", lambda: act.activation(out=sqp, in_=ps[6][0:64, :], func=AF.Square), reads=[PS[6]], writes=["sqp"])
            cx.op("pe", [lambda j=j, tt=tt: pe.matmul(ps[0][:, tt * 4 + j:tt * 4 + j + 1], lhsT=sqp[:, j * 128:(j + 1) * 128],
                                                      rhs=ones_bf[0:64, 0:1], start=True, stop=True) for j in range(4)],
                  reads=["sqp", "ones_bf"], writes=[PS[0]])
            cx.op("dve", lambda tt=tt: dve.tensor_copy(out=ssq_pe[:, tt * 4:tt * 4 + 4], in_=ps[0][:, tt * 4:tt * 4 + 4]),
                  reads=[PS[0]], writes=["ssq_pe"])
            cx.op("dve", lambda tt=tt: dve.tensor_tensor(out=kpr, in0=ps[6][0:64, :], in1=Ck[:, tt * 512:(tt + 1) * 512], op=OP.mult),
                  reads=[PS[6], "Ck"], writes=["kpr"])
            cx.op("dve", lambda tt=tt: dve.tensor_tensor(out=kpsw, in0=ps[7][0:64, :], in1=Sk[:, tt * 512:(tt + 1) * 512], op=OP.mult),
                  reads=[PS[7], "Sk"], writes=["kpsw"])
            cx.op("dve", lambda tt=tt: dve.tensor_tensor(out=kpe_rot[:, tt * 512:(tt + 1) * 512], in0=kpr, in1=kpsw, op=OP.add),
                  reads=["kpr", "kpsw"], writes=["kpe_rot"])
        dbg_out("d_kvn", kvn[:, :, :].rearrange("p a b -> p (a b)"), "kvn")
        dbg_out("d_kpe", kpe_rot[:, :], "kpe_rot")
        dbg_out("d_ssqpe", ssq_pe[:, :], "ssq_pe")
        cx.barrier(["dbg"])

        WS = dr["WS"]
        xo = dr["xTo"].rearrange("(k p) t -> p k t", p=128)
        ov = dr["outT"].rearrange("(k p) t -> p k t", p=128)
        gq0, _ = SM["gqn"]
        gk0, _ = SM["gkn"]
        nhalf = int(os.environ.get("MK_NHALF", "2"))
        stage = int(os.environ.get("MK_STAGE", "9"))

        for hf in range(nhalf):
            hsrcs = []
            _n = BG.nxt
            for _j in range(8):
                if _n < 192:
                    hsrcs.append(WAd[_n]); _n += 1
            for i in range(NSLOT_HALF):
                hsrcs.append(WS[i])
                if i < 8:
                    nb = 1
                elif i < 32:
                    nb = 3
                else:
                    nb = 0
                for _j in range(nb):
                    if _n < 192:
                        hsrcs.append(WAd[_n]); _n += 1
            strm = Stream(hsrcs)
            strm.fill()
            NKB = 8 * (hf + 1)
            NKT = NKB // 4
            tok0 = hf * 512
            d0 = (hf == 0)
            xT = view("A", 0, F32, [32, 512])
            hT = view("B", 0, BF16, [32, 512])
            oT = view("A", 0, BF16, [32, 512])
            gmT = view("A", 32768, BF16, [16, 512])
            qlatn = view("A", 49152, BF16, [8, 512])
            mixed = view("C", 0, BF16, [32, 512])
            acc = view("A", 0, F32, [32, 512])
            h2T = view("B", 0, BF16, [32, 512])

            if hf == 0:
                for kq in range(4):
                    cx.dma("sp", lambda kq=kq: sp.dma_start(out=xT[:, kq * 8:(kq + 1) * 8, :], in_=xo[:, kq * 8:(kq + 1) * 8, tok0:tok0 + 512]),
                           f"xT{kq}", writes=[RAK[kq]])
            hcalls = [0]

            def hhook():
                hcalls[0] += 1
                if hcalls[0] % 2 == 0:
                    bg_work(strm, 1, 6 + BG.nxt % 2)
            norm_stats(xT, RAK, sqv, ["sq0", "sq1"], 0, 32, hook=hhook)
            rsqrt_act(rstdv, "rstd", ps[0][:, :], PS[0], 1.0 / D)
            norm_apply(xT, RAK, hT, "hT", a1, "a1", 0, rstdv, "rstd", tmpv, ["tmp0", "tmp1"], hook=hhook)
            if d0:
                dbg_out("d_hT", hT.rearrange("p a b -> p (a b)"), "hT")
            cx.barrier(["dbg"])
            if stage < 2:
                continue

            qraw = view("C", 8192, F32, [8, 512])
            def q_ones(ft):
                cx.op("pe", lambda: pe.matmul(ps[3][:, :], lhsT=ones_bf[:, :], rhs=sqv[ft % 2], start=(ft == 0), stop=(ft == 7)),
                      reads=[f"sq{ft % 2}", "ones_bf"], writes=[PS[3]])

            for ft in range(8):
                wt, wk = strm.get()
                b = 1 + ft % 2
                cx.op("pe", [lambda k=k, wt=wt, b=b: pe.matmul(ps[b][:, :], lhsT=wt[:, k * 128:(k + 1) * 128], rhs=hT[:, k, :],
                                                               start=(k == 0), stop=(k == 31)) for k in range(32)],
                      reads=[wk, "hT"], writes=[PS[b]])
                if ft > 0:
                    q_ones(ft - 1)
                cx.op("act", lambda ft=ft, b=b: act.copy(out=qraw[:, ft, :], in_=ps[b][:, :]), reads=[PS[b]], writes=["qraw"])
                cx.op("act", lambda ft=ft, b=b: act.activation(out=sqv[ft % 2], in_=ps[b][:, :], func=AF.Square),
                      reads=[PS[b]], writes=[f"sq{ft % 2}"])
                bg_work(strm, 1, 7)
            q_ones(7)
            rsqrt_act(rstdv, "rstd", ps[3][:, :], PS[3], 1.0 / QL)
            for ft in range(8):
                cx.op("dve", lambda ft=ft: dve.scalar_tensor_tensor(out=qlatn[:, ft, :], in0=qraw[:, ft, :], scalar=smc("gql", ft),
                                                                    in1=rstdv, op0=OP.mult, op1=OP.mult),
                      reads=["qraw", "rstd", "smalls"], writes=["qlatn"])
            if d0:
                dbg_out("d_qlatn", qlatn.rearrange("p a b -> p (a b)"), "qlatn")
            cx.barrier(["dbg"])
            if stage < 3:
                continue

            kT = view("C", 0, BF16, [G, NKB * 128])
            vtok = view("C", 16384, BF16, [NKB, G * 128])
            o3 = 32768
            qTn = [view("A", o3 + i * 1024, BF16, [512]) for i in range(2)]
            qTr = [view("A", o3 + 2048 + i * 1024, BF16, [512])[0:64, :] for i in range(2)]
            sqk = [view("A", o3 + 4096 + i * 1024, BF16, [512]) for i in range(2)]
            sqr = view("A", o3 + 6144, BF16, [512])[0:64, :]
            pT = [view("A", o3 + 7168 + i * 1024, BF16, [512]) for i in range(3)]
            rks = view("A", o3 + 10240, F32, [G * 16])
            rq = view("A", 57344, F32, [512])
            r1 = view("A", 59392, F32, [512])[0:64, :]
            r2 = view("A", 61440, F32, [512])[0:64, :]
            rin = view("A", 63488, F32, [512])
            pTB = [view("A", o3 + 10752 + i * 1024, BF16, [512]) for i in range(3)]
            for hg in range(HEADS // G):
                wkvt, wkvk = strm.get()
                ktiles = [(hl_, kt_) for hl_ in range(G) for kt_ in range(NKT)]

                def k_mm(i):
                    hl_, kt_ = ktiles[i]
                    b = i % 2
                    cx.op("pe", [lambda k=k: pe.matmul(
                        ps[b][:, :], lhsT=wkvt[:, hl_ * 512 + k * 128:hl_ * 512 + (k + 1) * 128],
                        rhs=kvn[:, k, kt_ * 512:(kt_ + 1) * 512], start=(k == 0), stop=(k == 3)) for k in range(4)],
                        reads=[wkvk, "kvn"], writes=[PS[b]])
                    cx.op("dve", lambda: dve.tensor_scalar(out=kT[:, hl_, kt_ * 512:(kt_ + 1) * 512], in0=ps[b][:, :],
                                                           scalar1=smalls[:, gk0:gk0 + 1], scalar2=None, op0=OP.mult),
                          reads=[PS[b], "smalls"], writes=["kT"])
                    cx.op("act", lambda: act.activation(out=sqk[b], in_=ps[b][:, :], func=AF.Square),
                          reads=[PS[b]], writes=[f"sqk{b}"])

                def k_stats(i):
                    hl_, kt_ = ktiles[i]
                    b = i % 2
                    cx.op("pe", [lambda j=j: pe.matmul(
                        ps[2][:, hl_ * 16 + kt_ * 4 + j:hl_ * 16 + kt_ * 4 + j + 1], lhsT=sqk[b][:, j * 128:(j + 1) * 128],
                        rhs=ones_bf[:, 0:1], start=True, stop=True) for j in range(4)],
                        reads=[f"sqk{b}", "ones_bf"], writes=[PS[2]])

                for i in range(len(ktiles)):
                    k_mm(i)
                    if i > 0:
                        k_stats(i - 1)
                k_stats(len(ktiles) - 1)
                for hl in range(G):
                    cx.op("dve", lambda hl=hl: dve.tensor_tensor(out=rks[:, hl * 16:hl * 16 + NKB], in0=ps[2][:, hl * 16:hl * 16 + NKB],
                                                                 in1=ssq_pe[:, 0:NKB], op=OP.add),
                          reads=[PS[2], "ssq_pe"], writes=["rks"])
                    cx.op("act", lambda hl=hl: act.activation(out=rks[:, hl * 16:hl * 16 + NKB], in_=rks[:, hl * 16:hl * 16 + NKB],
                                                              func=AF.Ln, scale=1.0 / 192, bias=EPS), reads=["rks"], writes=["rks"])
                    cx.op("act", lambda hl=hl: act.activation(out=rks[:, hl * 16:hl * 16 + NKB], in_=rks[:, hl * 16:hl * 16 + NKB],
                                                              func=AF.Exp, scale=-0.5, bias=float(np.log(SCALE))),
                          reads=["rks"], writes=["rks"])
                for kb in range(NKB):
                    b = (3, 4, 0, 1)[kb % 4]
                    cx.op("pe", [lambda k=k, kb=kb, b=b: pe.matmul(
                        ps[b][:, :], lhsT=kvn[:, k, kb * 128:(kb + 1) * 128], rhs=wkvt[:, 2048 + k * 512:2048 + (k + 1) * 512],
                        start=(k == 0), stop=(k == 3)) for k in range(4)], reads=[wkvk, "kvn"], writes=[PS[b]])
                    cx.op("dve", lambda kb=kb, b=b: dve.tensor_copy(out=vtok[:, kb, :], in_=ps[b][:, :]), reads=[PS[b]], writes=["vtok"])
                bg_work(strm, 3, 5)
                def q_prod(hl, wqt, wqk):
                    qb = hl % 2
                    base = (hl % 2) * 2048
                    cx.op("pe", [lambda k=k: pe.matmul(
                        ps[0][:, :], lhsT=wqt[:, base + k * 256:base + k * 256 + 128], rhs=qlatn[:, k, :],
                        start=(k == 0), stop=(k == 7)) for k in range(8)], reads=[wqk, "qlatn"], writes=[PS[0]])
                    cx.op("pe", [lambda k=k: pe.matmul(
                        ps[1][0:64, :], lhsT=wqt[:, base + k * 256 + 128:base + k * 256 + 192], rhs=qlatn[:, k, :],
                        start=(k == 0), stop=(k == 7)) for k in range(8)], reads=[wqk, "qlatn"], writes=[PS[1]])
                    cx.op("pe", [lambda k=k: pe.matmul(
                        ps[3][0:64, :], lhsT=wqt[:, base + k * 256 + 192:base + k * 256 + 256], rhs=qlatn[:, k, :],
                        start=(k == 0), stop=(k == 7)) for k in range(8)], reads=[wqk, "qlatn"], writes=[PS[3]])
                    cx.op("act", lambda: act.activation(out=sqk[0], in_=ps[0][:, :], func=AF.Square), reads=[PS[0]], writes=["sqk0"])
                    cx.op("act", lambda: act.activation(out=sqr, in_=ps[1][0:64, :], func=AF.Square), reads=[PS[1]], writes=["sqr"])
                    cx.op("pe", [lambda: pe.matmul(ps[4][:, :], lhsT=ones_bf[:, :], rhs=sqk[0], start=True, stop=False),
                                 lambda: pe.matmul(ps[4][:, :], lhsT=ones_bf[0:64, :], rhs=sqr, start=False, stop=True)],
                          reads=["sqk0", "sqr", "ones_bf"], writes=[PS[4]])
                    rsqrt_act(rq, "rq", ps[4][:, :], PS[4], 1.0 / 192)
                    cx.op("dve", lambda: dve.scalar_tensor_tensor(out=qTn[qb], in0=ps[0][:, :], scalar=smalls[:, gq0:gq0 + 1],
                                                                  in1=rq, op0=OP.mult, op1=OP.mult),
                          reads=[PS[0], "rq", "smalls"], writes=[f"qTn{qb}"])
                    cx.op("dve", lambda: dve.tensor_tensor(out=r1, in0=ps[1][0:64, :], in1=Cq[:, tok0:tok0 + 512], op=OP.mult),
                          reads=[PS[1], "Cq"], writes=["r1"])
                    cx.op("dve", lambda: dve.tensor_tensor(out=r2, in0=ps[3][0:64, :], in1=Sq[:, tok0:tok0 + 512], op=OP.mult),
                          reads=[PS[3], "Sq"], writes=["r2"])
                    cx.op("dve", lambda: dve.tensor_tensor(out=r1, in0=r1, in1=r2, op=OP.add), reads=["r1", "r2"], writes=["r1"])
                    cx.op("dve", lambda: dve.tensor_tensor(out=qTr[qb], in0=r1, in1=rq[0:64, :], op=OP.mult),
                          reads=["r1", "rq"], writes=[f"qTr{qb}"])

                def jlo_of(kb):
                    return max(4 * hf, kb // 2) - 4 * hf

                def attn_pair(hls):
                    cfgs = []
                    for idx, hl_ in enumerate(hls):
                        cfgs.append(dict(hl=hl_, qb=hl_ % 2, h=hg * G + hl_, sb=(0, 1) if idx == 0 else (5, 6),
                                         ob=4 if idx == 0 else 7, smb=2 if idx == 0 else 3,
                                         pT=pT if idx == 0 else pTB, pk="pT" if idx == 0 else "pTB"))

                    def emit_scores(c, kb):
                        hl_, qb = c["hl"], c["qb"]
                        c0 = jlo_of(kb) * 128
                        b = c["sb"][kb % 2]
                        pi = kb % 3
                        pt = c["pT"][pi]
                        pk = f"{c['pk']}{pi}"
                        cx.op("pe", [lambda: pe.matmul(ps[b][:, c0:512], lhsT=kT[:, hl_, kb * 128:(kb + 1) * 128],
                                                       rhs=qTn[qb][:, c0:512], start=True, stop=False),
                                     lambda: pe.matmul(ps[b][:, c0:512], lhsT=kpe_rot[:, kb * 128:(kb + 1) * 128],
                                                       rhs=qTr[qb][:, c0:512], start=False, stop=True)],
                              reads=["kT", f"qTn{qb}", f"qTr{qb}", "kpe_rot"], writes=[PS[b]])
                        cx.op("act", lambda: act.activation(out=pt[:, c0:512], in_=ps[b][:, c0:512], func=AF.Exp,
                                                            scale=rks[:, hl_ * 16 + kb:hl_ * 16 + kb + 1], bias=negC[:, 0:1]),
                              reads=[PS[b], "rks", "negC"], writes=[pk])
                        Jm = kb // 2
                        if 4 * hf <= Jm < 4 * hf + 4:
                            jm = Jm - 4 * hf
                            mi = Jm * 2 + kb % 2
                            cx.op("dve", lambda: dve.tensor_tensor(out=pt[:, jm * 128:(jm + 1) * 128],
                                                                   in0=pt[:, jm * 128:(jm + 1) * 128],
                                                                   in1=maskb[:, mi * 128:(mi + 1) * 128], op=OP.mult),
                                  reads=[pk, "maskb"], writes=[pk])

                    def emit_pv(c, kb):
                        hl_ = c["hl"]
                        c0 = jlo_of(kb) * 128
                        pi = kb % 3
                        pt = c["pT"][pi]
                        pk = f"{c['pk']}{pi}"
                        ob, smb = c["ob"], c["smb"]
                        cx.op("pe", [lambda: pe.matmul(ps[ob][:, c0:512], lhsT=vtok[:, kb, hl_ * 128:(hl_ + 1) * 128],
                                                       rhs=pt[:, c0:512], start=(kb == 0), stop=(kb == NKB - 1),
                                                       skip_group_check=True),
                                     lambda: pe.matmul(ps[smb][:, c0:512], lhsT=ones_bf[:, :], rhs=pt[:, c0:512],
                                                       start=(kb == 0), stop=(kb == NKB - 1), skip_group_check=True)],
                              reads=["vtok", pk, "ones_bf"], writes=[PS[ob], PS[smb]])

                    for kb in range(NKB):
                        for c in cfgs:
                            emit_scores(c, kb)
                        if kb > 0:
                            for c in cfgs:
                                emit_pv(c, kb - 1)
                    for c in cfgs:
                        emit_pv(c, NKB - 1)
                    for c in cfgs:
                        ob, smb, h = c["ob"], c["smb"], c["h"]
                        cx.op("act", lambda smb=smb: act.activation(out=rin, in_=ps[smb][:, :], func=AF.Ln), reads=[PS[smb]], writes=["rin"])
                        cx.op("act", lambda: act.activation(out=rin, in_=rin, func=AF.Exp, scale=-1.0), reads=["rin"], writes=["rin"])
                        cx.op("dve", lambda ob=ob, h=h: dve.tensor_tensor(out=oT[:, h, :], in0=ps[ob][:, :], in1=rin, op=OP.mult),
                              reads=[PS[ob], "rin"], writes=["oT"])

                wq0, wq0k = strm.get()
                q_prod(0, wq0, wq0k)
                q_prod(1, wq0, wq0k)
                bg_work(strm, 3, 5)
                attn_pair((0, 1))
                wq1, wq1k = strm.get()
                q_prod(2, wq1, wq1k)
                q_prod(3, wq1, wq1k)
                bg_work(strm, 3, 5)
                attn_pair((2, 3))
            if d0:
                dbg_out("d_oT", oT.rearrange("p a b -> p (a b)"), "oT")
            cx.barrier(["dbg"])
            if stage < 4:
                continue

            vt = view("C", 0, BF16, [4, GMW])
            junk = view("C", 16384, BF16, [GMW])
            vss = view("C", 20480, F32, [8])
            mt = view("C", 20992, F32, [128])
            for ft in range(16):
                wt, wk = strm.get()
                b = ft % 2
                cx.op("pe", [lambda k=k, wt=wt, b=b: pe.matmul(ps[b][:, :], lhsT=wt[:, k * 128:(k + 1) * 128], rhs=hT[:, k, :],
                                                               start=(k == 0), stop=(k == 31)) for k in range(32)],
                      reads=[wk, "hT"], writes=[PS[b]])
                cx.op("act", lambda ft=ft, b=b: act.activation(out=gmT[:, ft, :], in_=ps[b][:, :], func=AF.Gelu),
                      reads=[PS[b]], writes=["gmT"])
            for cg in range(4):
                for s4 in range(4):
                    wt, wk = strm.get()
                    for tb in range(4):
                        cx.op("pe", [lambda kk=kk, wt=wt, tb=tb, s4=s4: pe.matmul(
                            ps[2 + tb][:, :], lhsT=hT[:, s4 * 8 + kk, tb * 128:(tb + 1) * 128], rhs=wt[:, kk * 512:(kk + 1) * 512],
                            start=(s4 == 0 and kk == 0), stop=(s4 == 3 and kk == 7)) for kk in range(8)],
                            reads=[wk, "hT"], writes=[PS[2 + tb]])
                for tb in range(4):
                    cx.op("act", lambda tb=tb, cg=cg: act.activation(out=vt[:, tb, cg * 512:(cg + 1) * 512], in_=ps[2 + tb][:, :],
                                                                     func=AF.Gelu), reads=[PS[2 + tb]], writes=["vt"])
            for tb in range(4):
                cx.op("act", lambda tb=tb: act.activation(out=junk, in_=vt[:, tb, :], func=AF.Square, accum_out=vss[:, tb:tb + 1]),
                      reads=["vt"], writes=["junk", "vss"])
            cx.op("act", lambda: act.activation(out=vss[:, 4:8], in_=vss[:, 0:4], func=AF.Ln, scale=1.0 / GMW, bias=EPS),
                  reads=["vss"], writes=["vss"])
            cx.op("act", lambda: act.activation(out=vss[:, 4:8], in_=vss[:, 4:8], func=AF.Exp, scale=-0.5), reads=["vss"], writes=["vss"])
            for tb in range(4):
                cx.op("dve", lambda tb=tb: dve.tensor_scalar(out=vt[:, tb, :], in0=vt[:, tb, :], scalar1=vss[:, 4 + tb:5 + tb],
                                                             scalar2=None, op0=OP.mult), reads=["vt", "vss"], writes=["vt"])
            for tb in range(4):
                for c in range(16):
                    g = c // 2
                    b = c % 2
                    cx.op("pe", lambda tb=tb, c=c, g=g, b=b: pe.matmul(ps[b][:, 0:128], lhsT=vt[:, tb, c * 128:(c + 1) * 128],
                                                                       rhs=wsm[:, g * 128:(g + 1) * 128], start=True, stop=True),
                          reads=["vt", "wsm"], writes=[PS[b]])
                    cx.op("dve", lambda c=c, g=g, b=b: dve.scalar_tensor_tensor(
                        out=mt, in0=ps[b][:, 0:128], scalar=smc("gv", c), in1=bsb[:, g * 128:(g + 1) * 128],
                        op0=OP.mult, op1=OP.add), reads=[PS[b], "smalls", "bsb"], writes=["mt"])
                    cx.op("dve", lambda tb=tb, c=c: dve.tensor_tensor(out=gmT[:, c, tb * 128:(tb + 1) * 128], in0=mt,
                                                                      in1=gmT[:, c, tb * 128:(tb + 1) * 128], op=OP.mult),
                          reads=["mt", "gmT"], writes=["gmT"])
            if d0:
                dbg_out("d_gm", gmT.rearrange("p a b -> p (a b)"), "gmT")
            cx.barrier(["dbg"])
            if stage < 5:
                continue

            sga = view("A", 49152, F32, [512])
            sgb = view("A", 51200, F32, [512])
            ya = view("A", 53248, F32, [512])
            yb = view("A", 55296, F32, [512])
            for cp in range(16):
                wa, wak = strm.get()
                for ci in range(2):
                    p0_ = 4 * ci
                    cx.op("pe", [lambda k=k, ci=ci, p0_=p0_, wa=wa: pe.matmul(
                        ps[p0_][:, :], lhsT=wa[:, ci * 2048 + k * 128:ci * 2048 + (k + 1) * 128], rhs=gmT[:, k, :],
                        start=(k == 0), stop=(k == 15)) for k in range(16)], reads=[wak, "gmT"], writes=[PS[p0_]])
                for ci in range(2):
                    c = cp * 2 + ci
                    p0_ = 4 * ci
                    for (src, skey, off) in [(oT, "oT", 1), (hT, "hT", 2), (hT, "hT", 3)]:
                        wt_, wk_ = strm.get()
                        cx.op("pe", [lambda k=k, wt_=wt_, src=src, off=off, p0_=p0_: pe.matmul(
                            ps[p0_ + off][:, :], lhsT=wt_[:, k * 128:(k + 1) * 128], rhs=src[:, k, :],
                            start=(k == 0), stop=(k == 31)) for k in range(32)], reads=[wk_, skey], writes=[PS[p0_ + off]])
                    cx.op("act", lambda p0_=p0_: act.activation(out=sga, in_=ps[p0_ + 2][:, :], func=AF.Sigmoid),
                          reads=[PS[p0_ + 2]], writes=["sga"])
                    cx.op("act", lambda p0_=p0_: act.activation(out=sgb, in_=ps[p0_ + 3][:, :], func=AF.Sigmoid),
                          reads=[PS[p0_ + 3]], writes=["sgb"])
                    cx.op("dve", lambda p0_=p0_: dve.tensor_tensor(out=ya, in0=ps[p0_][:, :], in1=sga, op=OP.mult),
                          reads=[PS[p0_], "sga"], writes=["ya"])
                    cx.op("dve", lambda p0_=p0_: dve.tensor_tensor(out=yb, in0=ps[p0_ + 1][:, :], in1=sgb, op=OP.mult),
                          reads=[PS[p0_ + 1], "sgb"], writes=["yb"])
                    cx.op("dve", lambda c=c: dve.tensor_tensor(out=mixed[:, c, :], in0=ya, in1=yb, op=OP.add),
                          reads=["ya", "yb"], writes=["mixed"])
            if d0:
                dbg_out("d_mixed", mixed.rearrange("p a b -> p (a b)"), "mixed")
            cx.barrier(["dbg"])
            if stage < 6:
                continue

            xs = [view("B", i * 8192, F32, [4, 512]) for i in range(2)]
            sq2 = [view("B", 16384 + i * 1024, BF16, [512]) for i in range(2)]
            def w_ones(n):
                cx.op("pe", lambda: pe.matmul(ps[2][:, :], lhsT=ones_bf[:, :], rhs=sq2[n % 2], start=(n == 0), stop=(n == 31)),
                      reads=[f"sq2_{n % 2}", "ones_bf"], writes=[PS[2]])

            for n in range(32):
                xi = (n // 4) % 2
                if n % 4 == 0:
                    cx.dma("sp", lambda n=n, xi=xi: sp.dma_start(out=xs[xi], in_=xo[:, n:n + 4, tok0:tok0 + 512]),
                           f"xs{xi}", writes=[f"xs{xi}"])
                wt, wk = strm.get()
                b = n % 2
                cx.op("pe", [lambda k=k, wt=wt, b=b: pe.matmul(ps[b][:, :], lhsT=wt[:, k * 128:(k + 1) * 128], rhs=mixed[:, k, :],
                                                               start=(k == 0), stop=(k == 31)) for k in range(32)],
                      reads=[wk, "mixed"], writes=[PS[b]])
                if n > 0:
                    w_ones(n - 1)
                cx.op("dve", lambda n=n, b=b, xi=xi: dve.scalar_tensor_tensor(
                    out=acc[:, n, :], in0=ps[b][:, :], scalar=mod[:, 64 + n:65 + n], in1=xs[xi][:, n % 4, :],
                    op0=OP.mult, op1=OP.add), reads=[PS[b], "mod_b", f"xs{xi}"], writes=[RAK[n // 8]])
                cx.op("act", lambda n=n: act.activation(out=sq2[n % 2], in_=acc[:, n, :], func=AF.Square),
                      reads=[RAK[n // 8]], writes=[f"sq2_{n % 2}"])
            w_ones(31)
            if d0:
                dbg_out("d_x1", acc.rearrange("p a b -> p (a b)"), RAK)
            cx.barrier(["dbg"])
            rsqrt_act(rstdv, "rstd", ps[2][:, :], PS[2], 1.0 / D)
            norm_apply(acc, RAK, h2T, "h2T", a2, "a2", 96, rstdv, "rstd", tmpv, ["tmp0", "tmp1"])
            if d0:
                dbg_out("d_h2", h2T.rearrange("p a b -> p (a b)"), "h2T")
            cx.barrier(["dbg"])
            if stage < 7:
                continue

            aT = [view("C", i * 8192, BF16, [8, 512]) for i in range(2)]
            rl = [view("C", 16384 + i * 2048, F32, [512]) for i in range(2)]
            for fb in range(16):
                ab = aT[fb % 2]
                abk = f"aT{fb % 2}"
                for ft in range(8):
                    wt, wk = strm.get()
                    b = ft % 2
                    cx.op("pe", [lambda k=k, wt=wt, b=b: pe.matmul(ps[b][:, :], lhsT=wt[:, k * 128:(k + 1) * 128], rhs=h2T[:, k, :],
                                                                   start=(k == 0), stop=(k == 31)) for k in range(32)],
                          reads=[wk, "h2T"], writes=[PS[b]])
                    cx.op("act", lambda b=b: act.activation(out=rl[b], in_=ps[b][:, :], func=AF.Relu), reads=[PS[b]], writes=[f"rl{b}"])
                    cx.op("dve", lambda ft=ft, b=b, ab=ab: dve.tensor_tensor(out=ab[:, ft, :], in0=rl[b], in1=rl[b], op=OP.mult),
                          reads=[f"rl{b}"], writes=[abk])
                for s8 in range(8):
                    wt, wk = strm.get()
                    for nn in range(4):
                        n = s8 * 4 + nn
                        b = 2 + n % 4
                        cx.op("pe", [lambda k=k, wt=wt, nn=nn, b=b, ab=ab: pe.matmul(
                            ps[b][:, :], lhsT=wt[:, nn * 1024 + k * 128:nn * 1024 + (k + 1) * 128], rhs=ab[:, k, :],
                            start=(k == 0), stop=(k == 7)) for k in range(8)], reads=[wk, abk], writes=[PS[b]])
                        cx.op("dve", lambda n=n, b=b: dve.scalar_tensor_tensor(
                            out=acc[:, n, :], in0=ps[b][:, :], scalar=mod[:, 160 + n:161 + n], in1=acc[:, n, :],
                            op0=OP.mult, op1=OP.add), reads=[PS[b], "mod_b", RAK[n // 8]], writes=[RAK[n // 8]])
            for kq in range(4):
                cx.dma("sp", lambda kq=kq: sp.dma_start(out=ov[:, kq * 8:(kq + 1) * 8, tok0:tok0 + 512], in_=acc[:, kq * 8:(kq + 1) * 8, :]),
                       f"out{kq}", reads=[RAK[kq]])
            if hf + 1 < nhalf:
                tn = (hf + 1) * 512
                for kq in range(4):
                    cx.dma("sp", lambda kq=kq, tn=tn: sp.dma_start(out=xT[:, kq * 8:(kq + 1) * 8, :], in_=xo[:, kq * 8:(kq + 1) * 8, tn:tn + 512]),
                           f"xT{kq}", writes=[RAK[kq]])
            cx.barrier(["dbg"])
        fin = [s for s in ("out0", "out1", "out2", "out3", "dbg") if s in cx.dsem]
        cx.final_wait("sp", fin)
        print(f"[build] waits={cx.nwait} pe={E['pe']['cnt']} act={E['act']['cnt']} dve={E['dve']['cnt']} "
              f"ring_dmas={rstate['issued']}")
    return nc


def _t1(W, col0, ncols):
    K = W.shape[0]
    blk = W[:, col0:col0 + ncols].reshape(K // 128, 128, ncols // 128, 128)
    return np.ascontiguousarray(blk.transpose(2, 1, 0, 3)).reshape(ncols // 128, 128, (K // 128) * 128)


_WCACHE = {}


def _prep_weights(w_ada, w_in, w_uq, w_ukv, w_branch_a, w_branch_b, w_out, w_ff1, w_ff2):
    WA = np.ascontiguousarray(w_ada.reshape(4, 8, 128, 48, 512).transpose(3, 0, 2, 1, 4)).reshape(192, 128, 4096)
    sw = (np.arange(64) + 32) % 64
    WKV = np.empty((5, 128, 4096), np.float32)
    WKV[0:4] = _t1(w_in, OFF_KV, 512)
    kp = w_in[:, OFF_KPE:OFF_KPE + 64]
    WKV[4] = _t1(np.concatenate([kp, kp[:, sw]], axis=1), 0, 128)[0]
    WS = np.empty((NSLOT_HALF, 128, 4096), np.float32)
    i = 0
    WS[i:i + 8] = _t1(w_in, OFF_Q, QL); i += 8
    wk4 = w_ukv.reshape(4, 128, HEADS, 256)
    wq8 = w_uq.reshape(8, 128, HEADS, 192)
    for hg in range(HEADS // G):
        hs = slice(hg * G, (hg + 1) * G)
        kpart = wk4[:, :, hs, 0:128].transpose(1, 2, 0, 3).reshape(128, G * 4 * 128)
        vpart = wk4[:, :, hs, 128:256].transpose(1, 0, 2, 3).reshape(128, 4 * G * 128)
        WS[i] = np.concatenate([kpart, vpart], axis=1); i += 1
        for hp in range(G // 2):
            parts = []
            for h2i in range(2):
                h = hg * G + hp * 2 + h2i
                q = wq8[:, :, h, :]
                blk = np.concatenate([q[:, :, 0:128], q[:, :, 128:192], q[:, :, 128:192][:, :, sw]], axis=2)
                parts.append(blk.transpose(1, 0, 2).reshape(128, 8 * 256))
            WS[i] = np.concatenate(parts, axis=1); i += 1
    WS[i:i + 16] = _t1(w_in, 0, GMW); i += 16
    for cg in range(4):
        blk = w_in[:, GMW + cg * 512:GMW + (cg + 1) * 512].reshape(4, 8, 128, 512)
        WS[i:i + 4] = blk.transpose(0, 2, 1, 3).reshape(4, 128, 4096); i += 4
    TA = _t1(w_branch_a, 0, D)
    TB = _t1(w_branch_b, 0, D)
    TGA = _t1(w_in, OFF_GATE, D)
    TGB = _t1(w_in, OFF_GATE + D, D)
    for cp in range(16):
        WS[i] = np.concatenate([TA[2 * cp], TA[2 * cp + 1]], axis=1); i += 1
        for ci in range(2):
            c = 2 * cp + ci
            WS[i] = TB[c]; WS[i + 1] = TGA[c]; WS[i + 2] = TGB[c]; i += 3
    WS[i:i + 32] = _t1(w_out, 0, D); i += 32
    T1f = _t1(w_ff1, 0, DFF)
    for fb in range(16):
        WS[i:i + 8] = T1f[fb * 8:(fb + 1) * 8]; i += 8
        blk = w_ff2[fb * 1024:(fb + 1) * 1024, :].reshape(8, 128, 8, 4, 128)
        WS[i:i + 8] = blk.transpose(2, 1, 3, 0, 4).reshape(8, 128, 4096); i += 8
    assert i == NSLOT_HALF, i
    return WA, WKV, WS


def kernel(x, c, positions, w_ada, b_ada, g_norm1, w_in, g_v, w_s, b_s, g_q_lat, g_kv_lat,
           w_uq, w_ukv, g_qn, g_kn, w_branch_a, w_branch_b, w_out, g_norm2, w_ff1, w_ff2):
    f = lambda a: np.asarray(a, dtype=np.float32)
    x = f(x); c = f(c); positions = np.asarray(positions, dtype=np.int32)
    WA, WKV, WS = _prep_weights(f(w_ada)[0], f(w_in)[0], f(w_uq)[0], f(w_ukv)[0], f(w_branch_a)[0],
                                f(w_branch_b)[0], f(w_out)[0], f(w_ff1)[0], f(w_ff2)[0])
    sw = (np.arange(64) + 32) % 64
    gq = f(g_qn)[0]; gk = f(g_kn)[0]
    invf = (1.0 / (np.float32(10000.0) ** (np.arange(0, 64, 2, dtype=np.float32) / np.float32(64)))).astype(np.float32)

    def gcol(g):
        o = np.zeros((128, 3), np.float32)
        o[:, 0] = g[0:128]; o[0:64, 1] = g[128:192]; o[0:64, 2] = g[128:192][sw]
        return o

    wsT = np.ascontiguousarray(f(w_s)[0].transpose(2, 0, 1)).reshape(128, 1024)
    tri = (np.arange(128)[:, None] <= np.arange(128)[None, :]).astype(np.float32)
    bsb = np.ascontiguousarray(np.broadcast_to(f(b_s)[0].reshape(1, 1024), (128, 1024)))
    grow = np.concatenate([gq, gk]).reshape(1, 384).astype(np.float32)
    in_maps = []
    for core in range(8):
        b, par = core // 2, core % 2
        own = np.concatenate([np.arange(blk * 128, (blk + 1) * 128) for blk in BL[par]])
        sm = np.zeros((128, NSM), np.float32)

        def put(name, arr):
            o, w = SM[name]
            sm[0:arr.shape[0], o:o + arr.shape[1]] = arr

        put("cT", c[b].reshape(32, 128).T)
        put("b_ada", f(b_ada)[0].reshape(192, 128).T)
        put("g1", f(g_norm1)[0].reshape(32, 128).T)
        put("g2", f(g_norm2)[0].reshape(32, 128).T)
        put("gql", f(g_q_lat)[0].reshape(8, 128).T)
        put("gkvl", f(g_kv_lat)[0].reshape(4, 128).T)
        put("gv", f(g_v)[0].reshape(16, 128).T)
        put("gqn", gcol(gq)); put("gkn", gcol(gk))
        put("invf", np.concatenate([invf, invf]).reshape(64, 1))
        put("sgn", np.concatenate([-np.ones(32), np.ones(32)]).astype(np.float32).reshape(64, 1))
        maskT = np.zeros((128, 16, 128), np.float32)
        for J in range(8):
            for m in range(2):
                kb = 2 * J + m
                Bq = BL[par][J]
                if kb < Bq:
                    maskT[:, J * 2 + m, :] = 1.0
                elif kb == Bq:
                    maskT[:, J * 2 + m, :] = tri
        in_maps.append({
            "xTb": np.ascontiguousarray(x[b].T), "xTo": np.ascontiguousarray(x[b][own].T),
            "posb": np.ascontiguousarray(np.broadcast_to(positions[b][None, :], (64, S))),
            "poso": np.ascontiguousarray(np.broadcast_to(positions[b][own][None, :], (64, 1024))),
            "smalls": sm, "grow": grow, "wsT": wsT, "tri": tri, "bsb": bsb,
            "maskT": maskT.reshape(128, 2048), "WA": WA, "WKV": WKV, "WS": WS,
        })
    nc = build_program()
    res = run_bass_kernel_spmd(nc, in_maps, core_ids=list(range(8)))
    out = np.empty((NB, S, D), np.float32)
    for core in range(8):
        b, par = core // 2, core % 2
        own = np.concatenate([np.arange(blk * 128, (blk + 1) * 128) for blk in BL[par]])
        out[b, own, :] = np.asarray(res.results[core]["outT"]).T
    kernel.last_results = res.results
    return out
```
